# Optimizing a Trainium2 kernel written in Bass

```python
import math
import jax
import jax.numpy as jnp
from jax import lax
import numpy as np

D_MODEL = 1024
BATCH = 4
SEQ = 8192
DEPTH = 2
DEC_BATCH = 32
DEC_SEQ = 8
PAST_LEN = 16384
PAGE_SIZE = 128

HEAD_DIM = 64
NSA_HEADS = D_MODEL // (2 * HEAD_DIM)
NSA_KV_HEADS = NSA_HEADS // 4
NSA_GROUP = NSA_HEADS // NSA_KV_HEADS
CMP_BLOCK = 32
SEL_BLOCK = 64
SEL_TOPN = 16
WINDOW = 512
Q_BLOCK = 128
ROPE_THETA = 10000.0
MLSTM_HEADS = D_MODEL // (4 * HEAD_DIM)
MLSTM_CONV = 4
MLSTM_CHUNK = 64
CMLP_GROUPS = D_MODEL // (4 * HEAD_DIM)
CMLP_CHUNK = 128
D_FF = 2816
FFN_CONV = 3
NSA_W = NSA_HEADS * HEAD_DIM
KV_W = NSA_KV_HEADS * HEAD_DIM
MLSTM_W = MLSTM_HEADS * HEAD_DIM
CMLP_W = CMLP_GROUPS * HEAD_DIM
IN_SIZES = (NSA_W, 2 * KV_W, 2 * KV_W, 2 * KV_W, 3 * NSA_HEADS, MLSTM_W, MLSTM_W, MLSTM_W, MLSTM_HEADS, MLSTM_HEADS, CMLP_W, CMLP_W)
IN_W = sum(IN_SIZES)
IN_SPLITS = tuple(int(v) for v in np.cumsum(IN_SIZES)[:-1])
FGATE_OFF = sum(IN_SIZES[:9])
EPS = 1e-6
NEG = -1e30
FORCE = 1e4

kernel_name = 'nsa_mlstm_chunkmlp_hybrid_step'


def rmsnorm(x, g):
    xf = x.astype(jnp.float32)
    y = xf * lax.rsqrt(jnp.mean(xf * xf, axis=-1, keepdims=True) + EPS)
    return (y * g.astype(jnp.float32)).astype(x.dtype)


def rope(x, pos):
    half = HEAD_DIM // 2
    inv = jnp.power(ROPE_THETA, -jnp.arange(half, dtype=jnp.float32) / half)
    ang = pos.astype(jnp.float32)[:, None] * inv[None, :]
    cos = jnp.cos(ang)[None, :, None, :]
    sin = jnp.sin(ang)[None, :, None, :]
    xf = x.astype(jnp.float32)
    x1, x2 = xf[..., :half], xf[..., half:]
    return jnp.concatenate([x1 * cos - x2 * sin, x1 * sin + x2 * cos], axis=-1).astype(x.dtype)


def causal_dwconv(x, hist, w, b):
    width = w.shape[0]
    S = x.shape[1]
    xp = jnp.concatenate([hist.astype(x.dtype), x], axis=1)
    y = b
    for j in range(width):
        y = y + xp[:, j:j + S] * w[j]
    return y, xp[:, xp.shape[1] - (width - 1):]


def masked_softmax(s, mask):
    p = jax.nn.softmax(jnp.where(mask, s, NEG), axis=-1)
    return p * mask.astype(p.dtype)


def nsa_compress(kv, a, w):
    B = kv.shape[0]
    nb = kv.shape[1] // CMP_BLOCK
    blk = kv[:, :nb * CMP_BLOCK].reshape(B, nb, CMP_BLOCK, 2, NSA_KV_HEADS, HEAD_DIM)
    pooled = jnp.einsum('bnlsgd,slgd->bnsgd', blk, a)
    return jnp.einsum('bnsgd,sgde->bnsge', pooled, w)


def nsa_attention(q, gates, kv_c, kv_s, kv_w, pos0, cmp_a, cmp_w):
    B, Sq = q.shape[0], q.shape[1]
    G, R, dh = NSA_KV_HEADS, NSA_GROUP, HEAD_DIM
    f32 = jnp.float32
    Tk = kv_s.shape[1]
    cmp = nsa_compress(kv_c.astype(f32), cmp_a.astype(f32), cmp_w.astype(f32))
    k_cmp, v_cmp = cmp[:, :, 0], cmp[:, :, 1]
    nbc = k_cmp.shape[1]
    cmp_end = (jnp.arange(nbc) + 1) * CMP_BLOCK - 1
    ratio = SEL_BLOCK // CMP_BLOCK
    nbs = -(-Tk // SEL_BLOCK)
    ntop = min(SEL_TOPN, nbs)
    sel = jnp.pad(kv_s, ((0, 0), (0, nbs * SEL_BLOCK - Tk), (0, 0), (0, 0), (0, 0)))
    sel = sel.reshape(B, nbs, SEL_BLOCK, 2, G, dh).transpose(0, 4, 1, 2, 3, 5).reshape(B, G, nbs, SEL_BLOCK * 2 * dh)
    blk = min(Q_BLOCK, Sq)
    nq = -(-Sq // blk)
    padq = nq * blk - Sq
    qg = jnp.pad(q, ((0, 0), (0, padq), (0, 0), (0, 0))).reshape(B, nq * blk, G, R, dh)
    gg = jnp.pad(gates, ((0, 0), (0, padq), (0, 0), (0, 0))).reshape(B, nq * blk, G, R, 3)
    kvw = jnp.pad(kv_w, ((0, 0), (0, padq), (0, 0), (0, 0), (0, 0)))
    blk_ids = jnp.arange(nbs)

    def one_block(i):
        start = i * blk
        qpos = pos0 + start + jnp.arange(blk)
        qb = lax.dynamic_slice_in_dim(qg, start, blk, axis=1).astype(f32) * (HEAD_DIM ** -0.5)
        gb = lax.dynamic_slice_in_dim(gg, start, blk, axis=1).astype(f32)
        s_c = jnp.einsum('bqgrd,bngd->bgrqn', qb, k_cmp)
        p_c = masked_softmax(s_c, cmp_end[None, :] <= qpos[:, None])
        o_c = jnp.einsum('bgrqn,bngd->bqgrd', p_c, v_cmp)
        imp = p_c.sum(axis=2)
        imp = jnp.pad(imp, ((0, 0), (0, 0), (0, 0), (0, nbs * ratio - nbc))).reshape(B, G, blk, nbs, ratio).sum(-1)
        forced = (blk_ids[None, :] == (qpos // SEL_BLOCK)[:, None]) | (blk_ids[None, :] == 0)
        valid = blk_ids[None, :] * SEL_BLOCK <= qpos[:, None]
        imp = jnp.where(valid, imp + FORCE * forced.astype(f32), NEG)
        top_v, top_i = lax.top_k(imp, ntop)
        top_ok = top_v > 0.5 * NEG
        gat = jnp.take_along_axis(sel, top_i.reshape(B, G, blk * ntop, 1), axis=2)
        gat = gat.reshape(B, G, blk, ntop, SEL_BLOCK, 2, dh).astype(f32)
        s_s = jnp.einsum('bqgrd,bgqnld->bgrqnl', qb, gat[..., 0, :]).reshape(B, G, R, blk, ntop * SEL_BLOCK)
        kpos = top_i[..., None] * SEL_BLOCK + jnp.arange(SEL_BLOCK)
        m_s = (top_ok[..., None] & (kpos <= qpos[None, None, :, None, None])).reshape(B, G, 1, blk, ntop * SEL_BLOCK)
        p_s = masked_softmax(s_s, m_s)
        o_s = jnp.einsum('bgrqk,bgqkd->bqgrd', p_s, gat[..., 1, :].reshape(B, G, blk, ntop * SEL_BLOCK, dh))
        wrow = lax.dynamic_slice_in_dim(kvw, start, blk + WINDOW, axis=1).astype(f32)
        wpos = pos0 - WINDOW + start + jnp.arange(blk + WINDOW)
        m_w = (wpos[None, :] <= qpos[:, None]) & (wpos[None, :] > qpos[:, None] - WINDOW) & (wpos[None, :] >= 0)
        s_w = jnp.einsum('bqgrd,bkgd->bgrqk', qb, wrow[:, :, 0])
        o_w = jnp.einsum('bgrqk,bkgd->bqgrd', masked_softmax(s_w, m_w), wrow[:, :, 1])
        return gb[..., 0:1] * o_c + gb[..., 1:2] * o_s + gb[..., 2:3] * o_w

    out = lax.map(one_block, jnp.arange(nq))
    out = jnp.moveaxis(out, 0, 1).reshape(B, nq * blk, NSA_W)[:, :Sq]
    return out.astype(q.dtype)


def mlstm_chunkwise(q, k, v, log_i, log_f, C0, n0, m0):
    B, S, H, dh = q.shape
    f32 = jnp.float32
    L = min(MLSTM_CHUNK, S)
    nc = -(-S // L)
    pad = nc * L - S

    def chunks4(t):
        t = jnp.pad(t.astype(f32), ((0, 0), (0, pad), (0, 0), (0, 0)))
        return t.reshape(B, nc, L, H, dh).transpose(1, 0, 3, 2, 4)

    def chunks3(t, fill):
        t = jnp.pad(t.astype(f32), ((0, 0), (0, pad), (0, 0)), constant_values=fill)
        return t.reshape(B, nc, L, H).transpose(1, 0, 3, 2)

    causal = jnp.tril(jnp.ones((L, L), dtype=bool))

    def step(carry, inp):
        C, n, m = carry
        qc, kc, vc, li, lf = inp
        b = jnp.cumsum(lf, axis=-1)
        D = jnp.where(causal, b[..., :, None] - b[..., None, :] + li[..., None, :], NEG)
        inter = b + m[..., None]
        mt = jnp.maximum(inter, D.max(-1))
        A = jnp.einsum('bhtd,bhsd->bhts', qc, kc) * jnp.exp(D - mt[..., None])
        wi = jnp.exp(inter - mt)
        num = wi[..., None] * jnp.einsum('bhvd,bhtd->bhtv', C, qc) + jnp.einsum('bhts,bhsv->bhtv', A, vc)
        den = jnp.maximum(jnp.abs(wi * jnp.einsum('bhd,bhtd->bht', n, qc) + A.sum(-1)), jnp.exp(-mt))
        h = num / den[..., None]
        bl = b[..., -1]
        g = bl[..., None] - b + li
        m_new = jnp.maximum(bl + m, g.max(-1))
        w = jnp.exp(g - m_new[..., None])
        dec = jnp.exp(bl + m - m_new)
        C_new = dec[..., None, None] * C + jnp.einsum('bhs,bhsv,bhsd->bhvd', w, vc, kc)
        n_new = dec[..., None] * n + jnp.einsum('bhs,bhsd->bhd', w, kc)
        return (C_new, n_new, m_new), h

    (C, n, m), hs = lax.scan(step, (C0.astype(f32), n0.astype(f32), m0.astype(f32)),
                             (chunks4(q), chunks4(k), chunks4(v), chunks3(log_i, NEG), chunks3(log_f, 0.0)))
    h = hs.transpose(1, 0, 3, 2, 4).reshape(B, nc * L, H, dh)[:, :S]
    return h, C, n, m


def chunk_mlp(u, v, w_s, b_s):
    B, S = u.shape[0], u.shape[1]
    nch = -(-S // CMLP_CHUNK)
    pad = nch * CMLP_CHUNK - S
    vc = jnp.pad(v, ((0, 0), (0, pad), (0, 0))).reshape(B, nch, CMLP_CHUNK, CMLP_GROUPS, HEAD_DIM)
    s = jnp.einsum('gts,bnsgc->bntgc', jnp.tril(w_s), vc) + b_s.T[None, None, :, :, None]
    s = s.reshape(B, nch * CMLP_CHUNK, CMLP_W)[:, :S]
    return u * s


def layer_forward(x, c, pos0, kv_past, win_past, mC, mn, mm, mconv, fconv, P):
    B, S = x.shape[0], x.shape[1]
    G, dh = NSA_KV_HEADS, HEAD_DIM
    mod = jnp.einsum('bd,de->be', jax.nn.silu(c), P['ada_w']) + P['ada_b']
    sh1, sc1, g1, sh2, sc2, g2 = jnp.split(mod[:, None, :], 6, axis=-1)
    h = rmsnorm(x, P['norm1_g']) * (1.0 + sc1) + sh1
    z = jnp.einsum('bsd,de->bse', h, P['w_in']) + P['b_in']
    q, kvc, kvs, kvw, gt, mqk, mv, mo, mi, mf, cu, cv = jnp.split(z, IN_SPLITS, axis=-1)
    pos = pos0 + jnp.arange(S, dtype=jnp.int32)

    q = rope(q.reshape(B, S, NSA_HEADS, dh), pos)

    def kv_rows(t):
        t = t.reshape(B, S, 2, G, dh)
        return jnp.stack([rope(t[:, :, 0], pos), t[:, :, 1]], axis=2)

    kvc, kvs, kvw = kv_rows(kvc), kv_rows(kvs), kv_rows(kvw)
    new_rows = jnp.concatenate([kvc, kvs], axis=2)
    if kv_past is None:
        full_c, full_s = kvc, kvs
        win_hist = jnp.zeros((B, WINDOW, 2, G, dh), kvw.dtype)
        n_keep = min(WINDOW, S)
    else:
        full = jnp.concatenate([kv_past.astype(new_rows.dtype), new_rows], axis=1)
        full_c, full_s = full[:, :, 0:2], full[:, :, 2:4]
        n_keep = win_past.shape[1]
        win_hist = jnp.pad(win_past.astype(kvw.dtype), ((0, 0), (WINDOW - n_keep, 0), (0, 0), (0, 0), (0, 0)))
    win_all = jnp.concatenate([win_hist, kvw], axis=1)
    new_win = win_all[:, win_all.shape[1] - n_keep:]
    gates = jax.nn.sigmoid(gt).reshape(B, S, NSA_HEADS, 3)
    o_nsa = nsa_attention(q, gates, full_c, full_s, win_all, pos0, P['cmp_a'], P['cmp_w'])

    qk_c, new_mconv = causal_dwconv(mqk, mconv, P['m_conv_w'], P['m_conv_b'])
    qk_c = jax.nn.silu(qk_c).reshape(B, S, MLSTM_HEADS, dh)
    mq = jnp.einsum('bshd,hde->bshe', qk_c, P['m_wq'])
    mk = jnp.einsum('bshd,hde->bshe', qk_c, P['m_wk']) * (dh ** -0.5)
    hm, C1, n1, m1 = mlstm_chunkwise(mq, mk, mv.reshape(B, S, MLSTM_HEADS, dh), mi,
                                     jax.nn.log_sigmoid(mf.astype(jnp.float32)), mC, mn, mm)
    hm = rmsnorm(hm, P['m_norm_g']).reshape(B, S, MLSTM_W)
    hm = (jax.nn.sigmoid(mo.astype(jnp.float32)) * hm).astype(x.dtype)

    cu = jax.nn.gelu(cu)
    cv = rmsnorm(jax.nn.gelu(cv), P['c_norm_g'])
    o_cm = chunk_mlp(cu, cv, P['c_ws'], P['c_bs'])

    mix = jnp.einsum('bse,ed->bsd', jnp.concatenate([o_nsa, hm, o_cm], axis=-1), P['w_out'])
    x = x + g1 * mix

    h2 = rmsnorm(x, P['norm2_g']) * (1.0 + sc2) + sh2
    up = jnp.einsum('bsd,df->bsf', h2, P['w_up'])
    up, new_fconv = causal_dwconv(up, fconv, P['f_conv_w'], P['f_conv_b'])
    a, bv = jnp.split(up, 2, axis=-1)
    x = x + g2 * jnp.einsum('bsf,fd->bsd', jax.nn.silu(a) * bv, P['w_down'])
    return x, (new_rows, new_win, C1, n1, m1, new_mconv, cv, new_fconv)


def setup_inputs(seed: int = 0) -> dict:
    key = jax.random.key(seed)
    ks = iter(jax.random.split(key, 40))
    f32 = jnp.float32

    def nrm(shape, s):
        return jax.random.normal(next(ks), shape, f32) * s

    G, dh = NSA_KV_HEADS, HEAD_DIM
    n_pages = PAST_LEN // PAGE_SIZE
    n_used = DEC_BATCH * n_pages
    n_pool = n_used + max(1, n_used // 4)
    wb = min(WINDOW, PAST_LEN)
    x_prompt = nrm((BATCH, SEQ, D_MODEL), 1.0)
    x_sample = nrm((DEC_BATCH, DEC_SEQ, D_MODEL), 1.0)
    cache_nsa_kv = nrm((DEPTH, n_pool, PAGE_SIZE, 4, G, dh), 1.0)
    page_table = jax.random.permutation(next(ks), n_pool)[:n_used].reshape(DEC_BATCH, n_pages).astype(jnp.int32)
    cache_win_kv = nrm((DEPTH, DEC_BATCH, wb, 2, G, dh), 1.0)
    state_mlstm_C = nrm((DEPTH, DEC_BATCH, MLSTM_HEADS, dh, dh), 0.1)
    state_mlstm_n = nrm((DEPTH, DEC_BATCH, MLSTM_HEADS, dh), 0.1)
    state_mlstm_m = nrm((DEPTH, DEC_BATCH, MLSTM_HEADS), 1.0)
    state_mlstm_conv = nrm((DEPTH, DEC_BATCH, MLSTM_CONV - 1, MLSTM_W), 1.0)
    state_ffn_conv = nrm((DEPTH, DEC_BATCH, FFN_CONV - 1, 2 * D_FF), 1.0)
    c_prompt = nrm((BATCH, D_MODEL), 1.0)
    c_sample = nrm((DEC_BATCH, D_MODEL), 1.0)
    ada_w = nrm((DEPTH, D_MODEL, 6 * D_MODEL), 0.5 * D_MODEL ** -0.5)
    ada_b = nrm((DEPTH, 6 * D_MODEL), 0.02)
    norm1_g = 1.0 + nrm((DEPTH, D_MODEL), 0.02)
    norm2_g = 1.0 + nrm((DEPTH, D_MODEL), 0.02)
    w_in = nrm((DEPTH, D_MODEL, IN_W), D_MODEL ** -0.5)
    b_in = nrm((DEPTH, IN_W), 0.02).at[:, FGATE_OFF:FGATE_OFF + MLSTM_HEADS].add(3.0)
    cmp_a = nrm((DEPTH, 2, CMP_BLOCK, G, dh), CMP_BLOCK ** -0.5)
    cmp_w = nrm((DEPTH, 2, G, dh, dh), dh ** -0.5)
    m_conv_w = nrm((DEPTH, MLSTM_CONV, MLSTM_W), MLSTM_CONV ** -0.5)
    m_conv_b = nrm((DEPTH, MLSTM_W), 0.02)
    m_wq = nrm((DEPTH, MLSTM_HEADS, dh, dh), dh ** -0.5)
    m_wk = nrm((DEPTH, MLSTM_HEADS, dh, dh), dh ** -0.5)
    m_norm_g = 1.0 + nrm((DEPTH, MLSTM_HEADS, dh), 0.02)
    c_norm_g = 1.0 + nrm((DEPTH, CMLP_W), 0.02)
    c_ws = nrm((DEPTH, CMLP_GROUPS, CMLP_CHUNK, CMLP_CHUNK), CMLP_CHUNK ** -0.5)
    c_bs = 1.0 + nrm((DEPTH, CMLP_GROUPS, CMLP_CHUNK), 0.02)
    w_out = nrm((DEPTH, D_MODEL, D_MODEL), D_MODEL ** -0.5)
    w_up = nrm((DEPTH, D_MODEL, 2 * D_FF), D_MODEL ** -0.5)
    f_conv_w = nrm((DEPTH, FFN_CONV, 2 * D_FF), FFN_CONV ** -0.5)
    f_conv_b = nrm((DEPTH, 2 * D_FF), 0.02)
    w_down = nrm((DEPTH, D_FF, D_MODEL), D_FF ** -0.5)
    final_norm_g = 1.0 + nrm((D_MODEL,), 0.02)
    return {'x_prompt': x_prompt, 'x_sample': x_sample, 'cache_nsa_kv': cache_nsa_kv, 'page_table': page_table,
            'cache_win_kv': cache_win_kv, 'state_mlstm_C': state_mlstm_C, 'state_mlstm_n': state_mlstm_n,
            'state_mlstm_m': state_mlstm_m, 'state_mlstm_conv': state_mlstm_conv, 'state_ffn_conv': state_ffn_conv,
            'c_prompt': c_prompt, 'c_sample': c_sample, 'ada_w': ada_w, 'ada_b': ada_b, 'norm1_g': norm1_g,
            'norm2_g': norm2_g, 'w_in': w_in, 'b_in': b_in, 'cmp_a': cmp_a, 'cmp_w': cmp_w, 'm_conv_w': m_conv_w,
            'm_conv_b': m_conv_b, 'm_wq': m_wq, 'm_wk': m_wk, 'm_norm_g': m_norm_g, 'c_norm_g': c_norm_g,
            'c_ws': c_ws, 'c_bs': c_bs, 'w_out': w_out, 'w_up': w_up, 'f_conv_w': f_conv_w, 'f_conv_b': f_conv_b,
            'w_down': w_down, 'final_norm_g': final_norm_g}


def reference(x_prompt, x_sample, cache_nsa_kv, page_table, cache_win_kv, state_mlstm_C, state_mlstm_n,
              state_mlstm_m, state_mlstm_conv, state_ffn_conv, c_prompt, c_sample, ada_w, ada_b, norm1_g,
              norm2_g, w_in, b_in, cmp_a, cmp_w, m_conv_w, m_conv_b, m_wq, m_wk, m_norm_g, c_norm_g, c_ws,
              c_bs, w_out, w_up, f_conv_w, f_conv_b, w_down, final_norm_g):
    f32 = jnp.float32
    Bp = x_prompt.shape[0]
    Bs = x_sample.shape[0]
    n_pages = page_table.shape[1]
    past_len = n_pages * PAGE_SIZE
    xp, xs = x_prompt, x_sample
    st_p, st_s = [], []
    for l in range(DEPTH):
        P = {'ada_w': ada_w[l], 'ada_b': ada_b[l], 'norm1_g': norm1_g[l], 'norm2_g': norm2_g[l],
             'w_in': w_in[l], 'b_in': b_in[l], 'cmp_a': cmp_a[l], 'cmp_w': cmp_w[l], 'm_conv_w': m_conv_w[l],
             'm_conv_b': m_conv_b[l], 'm_wq': m_wq[l], 'm_wk': m_wk[l], 'm_norm_g': m_norm_g[l],
             'c_norm_g': c_norm_g[l], 'c_ws': c_ws[l], 'c_bs': c_bs[l], 'w_out': w_out[l], 'w_up': w_up[l],
             'f_conv_w': f_conv_w[l], 'f_conv_b': f_conv_b[l], 'w_down': w_down[l]}
        xp, sp = layer_forward(
            xp, c_prompt, 0, None, None,
            jnp.zeros((Bp, MLSTM_HEADS, HEAD_DIM, HEAD_DIM), f32), jnp.zeros((Bp, MLSTM_HEADS, HEAD_DIM), f32),
            jnp.zeros((Bp, MLSTM_HEADS), f32), jnp.zeros((Bp, MLSTM_CONV - 1, MLSTM_W), xp.dtype),
            jnp.zeros((Bp, FFN_CONV - 1, 2 * D_FF), xp.dtype), P)
        kv_past = cache_nsa_kv[l][page_table].reshape(Bs, past_len, 4, NSA_KV_HEADS, HEAD_DIM)
        xs, ss = layer_forward(
            xs, c_sample, past_len, kv_past, cache_win_kv[l], state_mlstm_C[l], state_mlstm_n[l],
            state_mlstm_m[l], state_mlstm_conv[l], state_ffn_conv[l], P)
        st_p.append(sp)
        st_s.append(ss)
    sp = [jnp.stack(t) for t in zip(*st_p)]
    ss = [jnp.stack(t) for t in zip(*st_s)]
    y_prompt = rmsnorm(xp, final_norm_g)
    y_sample = rmsnorm(xs, final_norm_g)
    kv_rows_prompt, win_prompt, mlstm_C_prompt, mlstm_n_prompt, mlstm_m_prompt, mlstm_conv_prompt, ffn_conv_prompt = sp[0], sp[1], sp[2], sp[3], sp[4], sp[5], sp[7]
    kv_rows_sample, win_sample, mlstm_C_sample, mlstm_n_sample, mlstm_m_sample, mlstm_conv_sample, cmlp_v_sample, ffn_conv_sample = ss[0], ss[1], ss[2], ss[3], ss[4], ss[5], ss[6], ss[7]
    return (y_prompt, y_sample, kv_rows_prompt, kv_rows_sample, win_prompt, win_sample,
            mlstm_C_prompt, mlstm_n_prompt, mlstm_m_prompt, mlstm_conv_prompt,
            mlstm_C_sample, mlstm_n_sample, mlstm_m_sample, mlstm_conv_sample,
            cmlp_v_sample, ffn_conv_prompt, ffn_conv_sample)
```

```python
import numpy as np
import concourse.bass as bass
import concourse.mybir as mybir

F32 = mybir.dt.float32
BF16 = mybir.dt.bfloat16
I32 = mybir.dt.int32
AF = mybir.ActivationFunctionType
ALU = mybir.AluOpType
AX = mybir.AxisListType


class V:
    __slots__ = ("o", "ap")

    def __init__(self, o, ap):
        self.o = o
        self.ap = ap

    def __getitem__(self, idx):
        return V(self.o, self.ap[idx])

    def rearrange(self, *a, **k):
        return V(self.o, self.ap.rearrange(*a, **k))

    def bitcast(self, dt):
        return V(self.o, self.ap.bitcast(dt))

    def bc(self, shape):
        return V(self.o, self.ap.to_broadcast(list(shape)))

    def unsqueeze(self, d):
        return V(self.o, self.ap.unsqueeze(d))

    @property
    def shape(self):
        return self.ap.shape


class T:
    __slots__ = ("t", "w", "r", "name", "ro", "psum")

    def __init__(self, t, name):
        self.t = t
        self.name = name
        self.w = None
        self.r = {}
        self.ro = False
        self.psum = False

    def __getitem__(self, idx):
        return V(self, self.t[idx])

    def sub(self, name=None):
        return T(self.t, name or self.name + "_s")


def _ap(x):
    return x.ap if isinstance(x, V) else x


def _own(xs):
    out = []
    for x in xs:
        if isinstance(x, V):
            out.append(x.o)
        elif isinstance(x, T):
            out.append(x)
    return out


class MK:
    NDMA = 48

    def __init__(self, nc, stack):
        self.nc = nc
        self.stack = stack
        self.eng = {"pe": nc.tensor, "act": nc.scalar, "dve": nc.vector, "pool": nc.gpsimd, "sp": nc.sync}
        self.sems = {}
        self.cnt = {}
        for k in ["pe", "act", "dve", "pool"]:
            self.sems[k] = stack.enter_context(nc.semaphore("s_" + k))
            self.cnt[k] = 0
        self.dsem = []
        for i in range(self.NDMA):
            k = "d%d" % i
            self.sems[k] = stack.enter_context(nc.semaphore("s_" + k))
            self.cnt[k] = 0
            self.dsem.append(k)
        self.dnext = {"hw": 0, "sw": 0}
        self.NSW = 16
        self.seen = {e: {} for e in self.eng}
        self.ntile = 0
        self.ninst = 0
        self.dead = False

    def sb(self, shape, dt=F32, name=None, stack=None):
        self.ntile += 1
        name = (name or "t") + "_%d" % self.ntile
        t = (stack or self.stack).enter_context(self.nc.sbuf_tensor(name, list(shape), dt))
        return T(t, name)

    def ps(self, shape, dt=F32, name=None):
        self.ntile += 1
        name = (name or "p") + "_%d" % self.ntile
        t = self.stack.enter_context(self.nc.psum_tensor(name, list(shape), dt))
        tt_ = T(t, name)
        tt_.psum = True
        return tt_

    def _wait(self, e, k, v):
        if self.seen[e].get(k, 0) >= v:
            return
        self.eng[e].wait_ge(self.sems[k], v)
        self.seen[e][k] = v
        self.ninst += 1

    def _deps(self, e, reads, writes):
        for t in reads:
            if t.w is not None:
                self._wait(e, *t.w)
        for t in writes:
            if t.w is not None:
                self._wait(e, *t.w)
            for k, v in t.r.items():
                self._wait(e, k, v)

    def _mark(self, ev, reads, writes):
        k, v = ev
        for t in reads:
            if not t.ro:
                t.r[k] = v
        for t in writes:
            t.w = ev
            t.r = {}

    def op(self, e, fn, reads=(), writes=()):
        if self.dead:
            return None
        reads = _own(reads)
        writes = _own(writes)
        writes = writes + [x for x in reads if x.psum and x not in writes]
        self._deps(e, reads, writes)
        ins = fn()
        self.cnt[e] += 1
        ins.then_inc(self.sems[e], 1)
        ev = (e, self.cnt[e])
        if e == "pe":
            self.seen[e][e] = self.cnt[e]
        self._mark(ev, reads, writes)
        self.ninst += 1
        return ev

    def dma(self, q, out, in_, reads=(), writes=(), **kw):
        if self.dead:
            return None
        reads = _own(list(reads) + [in_])
        writes = _own(list(writes) + [out])
        if q == "pool":
            k = self.dsem[self.dnext["sw"]]
            self.dnext["sw"] = (self.dnext["sw"] + 1) % self.NSW
        else:
            k = self.dsem[self.NSW + self.dnext["hw"]]
            self.dnext["hw"] = (self.dnext["hw"] + 1) % (self.NDMA - self.NSW)
        if self.cnt[k] > 0:
            self._wait(q, k, self.cnt[k])
        self._deps(q, reads, writes)
        ins = self.eng[q].dma_start(out=_ap(out), in_=_ap(in_), **kw)
        self.cnt[k] += 16
        ins.then_inc(self.sems[k], 16)
        ev = (k, self.cnt[k])
        self._mark(ev, reads, writes)
        self.ninst += 1
        return ev

    def barrier(self):
        if self.dead:
            return
        for e in ["pe", "act", "dve", "pool", "sp"]:
            for k in self.dsem:
                if self.cnt[k] > 0:
                    self._wait(e, k, self.cnt[k])
            for k in ["pe", "act", "dve", "pool"]:
                if k != e and self.cnt[k] > 0:
                    self._wait(e, k, self.cnt[k])

    def finish(self):
        for k in self.dsem:
            if self.cnt[k] > 0:
                self._wait("sp", k, self.cnt[k])
        for e in ["pe", "act", "dve", "pool"]:
            if self.cnt[e] > 0:
                self._wait("sp", e, self.cnt[e])

    def mm(self, out, lhsT, rhs, start=True, stop=True):
        return self.op("pe", lambda: self.nc.tensor.matmul(_ap(out), _ap(lhsT), _ap(rhs), start=start, stop=stop,
                                                           skip_group_check=True),
                       [lhsT, rhs], [out])

    def tr(self, out, in_, ident):
        return self.op("pe", lambda: self.nc.tensor.transpose(_ap(out), _ap(in_), _ap(ident)), [in_, ident], [out])

    def act(self, out, in_, func, bias=None, scale=1.0, accum_out=None):
        kw = {}
        rd = [in_]
        wr = [out]
        if bias is not None:
            kw["bias"] = _ap(bias)
            rd.append(bias)
        if isinstance(scale, V):
            rd.append(scale)
        kw["scale"] = _ap(scale)
        if accum_out is not None:
            kw["accum_out"] = _ap(accum_out)
            wr.append(accum_out)
        return self.op("act", lambda: self.nc.scalar.activation(out=_ap(out), in_=_ap(in_), func=func, **kw), rd, wr)

    def tt(self, e, out, in0, in1, op):
        return self.op(e, lambda: self.eng[e].tensor_tensor(out=_ap(out), in0=_ap(in0), in1=_ap(in1), op=op),
                       [in0, in1], [out])

    def ts(self, e, out, in0, s1, op0, s2=None, op1=None, accum_out=None):
        rd = [in0, s1, s2]
        wr = [out]
        kw = {}
        if op1 is not None:
            kw["op1"] = op1
        if accum_out is not None:
            kw["accum_out"] = _ap(accum_out)
            wr.append(accum_out)
        return self.op(e, lambda: self.eng[e].tensor_scalar(out=_ap(out), in0=_ap(in0), scalar1=_ap(s1), scalar2=_ap(s2),
                                                            op0=op0, **kw), rd, wr)

    def stt(self, out, in0, scalar, in1, op0, op1):
        return self.op("dve", lambda: self.nc.vector.scalar_tensor_tensor(out=_ap(out), in0=_ap(in0), scalar=_ap(scalar),
                                                                           in1=_ap(in1), op0=op0, op1=op1),
                       [in0, scalar, in1], [out])

    def copy(self, e, out, in_):
        if e == "act":
            return self.op("act", lambda: self.nc.scalar.copy(out=_ap(out), in_=_ap(in_)), [in_], [out])
        return self.op(e, lambda: self.eng[e].tensor_copy(out=_ap(out), in_=_ap(in_)), [in_], [out])

    def memset(self, e, out, val):
        return self.op(e, lambda: self.eng[e].memset(_ap(out), val), [], [out])

    def recip(self, out, in_):
        return self.op("dve", lambda: self.nc.vector.reciprocal(out=_ap(out), in_=_ap(in_)), [in_], [out])

    def reduce(self, out, in_, op, axis=AX.X):
        return self.op("dve", lambda: self.nc.vector.tensor_reduce(out=_ap(out), in_=_ap(in_), axis=axis, op=op),
                       [in_], [out])

    def scan(self, out, d0, d1, initial, op0, op1):
        return self.op("dve", lambda: self.nc.vector.tensor_tensor_scan(out=_ap(out), data0=_ap(d0), data1=_ap(d1),
                                                                         initial=_ap(initial), op0=op0, op1=op1),
                       [d0, d1, initial], [out])

    def idma(self, out, in_, idx):
        if self.dead:
            return None
        reads = _own([idx])
        writes = _own([out])
        k = self.dsem[self.dnext["sw"]]
        self.dnext["sw"] = (self.dnext["sw"] + 1) % self.NSW
        if self.cnt[k] > 0:
            self._wait("pool", k, self.cnt[k])
        self._deps("pool", reads, writes)
        ins = self.nc.gpsimd.indirect_dma_start(out=_ap(out), out_offset=None, in_=_ap(in_),
                                                in_offset=bass.IndirectOffsetOnAxis(ap=_ap(idx), axis=0))
        self.cnt[k] += 16
        ins.then_inc(self.sems[k], 16)
        ev = (k, self.cnt[k])
        self._mark(ev, reads, writes)
        self.ninst += 1
        return ev


import math
from contextlib import ExitStack
import numpy as np
import concourse.bass as bass
import concourse.mybir as mybir

D = 1024
DFF = 2816
NEGM = -30000.0
EPS = 1e-6

UNITS = []
for h in range(8):
    UNITS.append(("q%d" % h, h * 64, 64))
for bi, bn in enumerate(["c", "s", "w"]):
    base = 512 + 256 * bi
    UNITS.append(("k%s0" % bn, base, 64))
    UNITS.append(("k%s1" % bn, base + 64, 64))
    UNITS.append(("v%s" % bn, base + 128, 128))
UNITS.append(("gt", 1280, 24))
for nm, base in [("mqk", 1304), ("mv", 1560), ("mo", 1816)]:
    UNITS.append((nm + "0", base, 128))
    UNITS.append((nm + "1", base + 128, 128))
UNITS.append(("mi", 2072, 4))
UNITS.append(("mf", 2076, 4))
for nm, base in [("cu", 2080), ("cv", 2336)]:
    UNITS.append((nm + "0", base, 128))
    UNITS.append((nm + "1", base + 128, 128))
NU = len(UNITS)
UIDX = {u[0]: i for i, u in enumerate(UNITS)}


class Cfg:
    def __init__(self, SEQ=8192, PAST=16384, NSS=4, NPOOL=5120, DEPTH=2, do_sample=True, do_prompt=True):
        self.SEQ, self.PAST, self.NSS, self.NPOOL, self.DEPTH = SEQ, PAST, NSS, NPOOL, DEPTH
        self.NT = SEQ // 128
        self.NG = SEQ // 512
        self.NPG = PAST // 128
        self.do_sample = do_sample
        self.do_prompt = do_prompt


def host_consts(cfg):
    c = {}
    half = 32
    inv = np.power(np.float32(10000.0), -np.arange(half, dtype=np.float32) / np.float32(half)).astype(np.float32)
    pos = np.concatenate([np.arange(cfg.SEQ), cfg.PAST + np.arange(8)]).astype(np.float32)
    ang = pos[None, :] * inv[:, None]
    cos = np.cos(ang).astype(np.float32)
    sin = np.sin(ang).astype(np.float32)
    c["ropecos"] = np.ascontiguousarray(np.concatenate([cos, cos], 0))
    c["ropesin"] = np.ascontiguousarray(np.concatenate([sin, sin], 0))
    c["ident"] = np.eye(128, dtype=np.float32)
    pr = np.zeros((64, 64), np.float32)
    for d in range(32):
        pr[d + 32, d] = -1.0
        pr[d, d + 32] = 1.0
    c["prot"] = pr
    k = np.arange(128)[:, None]
    q = np.arange(128)[None, :]
    c["caust"] = np.where(k <= q, 0.0, NEGM).astype(np.float32)
    c["causu"] = np.where(k > q, 0.0, NEGM).astype(np.float32)
    qq = np.arange(128)[:, None]
    jj = np.arange(4)[None, :]
    c["cmask"] = np.where(32 * jj + 31 <= qq, 0.0, NEGM).astype(np.float32)
    mul = np.ones((128, 2), np.float32)
    add = np.zeros((128, 2), np.float32)
    lo = np.arange(128) < 64
    add[lo, 0] = 1e4
    mul[lo, 1] = 0.0
    add[lo, 1] = -1e30
    add[~lo, 1] = 1e4
    c["impmul"] = mul
    c["impadd"] = add
    e = np.zeros((128, 8192), np.float32)
    kk = np.arange(8192)
    e[kk // 64, kk] = 1.0
    c["efull"] = e
    blk = np.zeros((128, 4), np.float32)
    blk[np.arange(128), np.arange(128) // 32] = 1.0
    c["blk"] = blk
    sel = np.zeros((4, 4, 128), np.float32)
    for h in range(4):
        sel[h, h, :] = 1.0
    c["selh"] = sel
    c["iota"] = np.arange(128, dtype=np.float32).reshape(128, 1)
    return c


class Builder:
    def __init__(self, cfg):
        self.cfg = cfg
        self.nc = bass.Bass("TRN2", target_bir_lowering=False)
        self.ins = {}
        self.outs = {}

    def din(self, name, shape, dt=F32):
        t = self.nc.dram_tensor(name, list(shape), dt, kind="ExternalInput")
        self.ins[name] = t
        return t.ap()

    def dout(self, name, shape, dt=F32):
        t = self.nc.dram_tensor(name, list(shape), dt, kind="ExternalOutput")
        self.outs[name] = t
        return t.ap()

    def dscr(self, name, shape, dt=BF16):
        return self.nc.dram_tensor(name, list(shape), dt, kind="Internal").ap()


def build_program(cfg):
    B = Builder(cfg)
    nc = B.nc
    L, NSS, SEQ, NT, NG = cfg.DEPTH, cfg.NSS, cfg.SEQ, cfg.NT, cfg.NG
    NR = 1 + NSS
    NTOK_S = NSS * 8
    xp = B.din("xp", [SEQ, D])
    xs = B.din("xs", [NTOK_S, D])
    cache = B.din("cache", [L, cfg.NPOOL * 128, 512])
    ptab = B.din("ptab", [NSS, cfg.NPG], I32)
    wkv = B.din("wkv", [L, NSS, 512, 256])
    sC = B.din("sC", [L, NSS, 4, 64, 64])
    sn = B.din("sn", [L, NSS, 4, 64])
    smm = B.din("sm", [L, NSS, 4])
    sconv = B.din("sconv", [L, NSS, 3, 256])
    sfconv = B.din("sfconv", [L, NSS, 2, 2 * DFF])
    cc = B.din("cc", [NR, D])
    ada_w = B.din("ada_w", [L, D, 6 * D])
    ada_b = B.din("ada_b", [L, 6 * D])
    norm1_g = B.din("norm1_g", [L, D])
    norm2_g = B.din("norm2_g", [L, D])
    w_in = B.din("w_in", [L, D, 2592])
    b_in = B.din("b_in", [L, 2592])
    cmp_a = B.din("cmp_a", [L, 2, 32, 2, 64])
    cmp_w = B.din("cmp_w", [L, 2, 2, 64, 64])
    m_conv_w = B.din("m_conv_w", [L, 4, 256])
    m_conv_b = B.din("m_conv_b", [L, 256])
    m_wq = B.din("m_wq", [L, 4, 64, 64])
    m_wk = B.din("m_wk", [L, 4, 64, 64])
    m_norm_g = B.din("m_norm_g", [L, 256])
    c_norm_g = B.din("c_norm_g", [L, 256])
    c_ws = B.din("c_ws", [L, 4, 128, 128])
    c_bs = B.din("c_bs", [L, 4, 128])
    w_out = B.din("w_out", [L, D, D])
    w_up = B.din("w_up", [L, D, 2 * DFF])
    f_conv_w = B.din("f_conv_w", [L, 3, 2 * DFF])
    f_conv_b = B.din("f_conv_b", [L, 2 * DFF])
    w_down = B.din("w_down", [L, DFF, D])
    fng_d = B.din("final_norm_g", [D])
    hc = host_consts(cfg)
    cd = {k: B.din("c_" + k, list(v.shape)) for k, v in hc.items()}
    NPOS = hc["ropecos"].shape[1]
    y_p = B.dout("y_p", [SEQ, D])
    y_s = B.dout("y_s", [NTOK_S, D])
    kvr_p = B.dout("kvr_p", [L, SEQ, 512])
    kvr_s = B.dout("kvr_s", [L, NTOK_S, 512])
    win_p = B.dout("win_p", [L, 512, 256])
    win_s = B.dout("win_s", [L, NSS, 512, 256])
    mC_p = B.dout("mC_p", [L, 4, 64, 64])
    mn_p = B.dout("mn_p", [L, 4, 64])
    mm_p = B.dout("mm_p", [L, 4])
    mconv_p = B.dout("mconv_p", [L, 3, 256])
    mC_s = B.dout("mC_s", [L, NSS, 4, 64, 64])
    mn_s = B.dout("mn_s", [L, NSS, 4, 64])
    mm_s = B.dout("mm_s", [L, NSS, 4])
    mconv_s = B.dout("mconv_s", [L, NSS, 3, 256])
    cv_s = B.dout("cv_s", [L, NTOK_S, 256])
    fconv_p = B.dout("fconv_p", [L, 2, 2 * DFF])
    fconv_s = B.dout("fconv_s", [L, NSS, 2, 2 * DFF])
    w_in_s = B.dscr("w_in_s", [L, NU, 128, 8, 128])
    w_out_s = B.dscr("w_out_s", [L, 8, 128, 8, 128])
    w_up_s = B.dscr("w_up_s", [L, 44, 128, 8, 128])
    w_down_s = B.dscr("w_down_s", [L, DFF, D])
    xscr = B.dscr("xscr", [NG, 128, 8, 512], F32)
    xscr_s = B.dscr("xscr_s", [128, 8, NTOK_S], F32)

    st = ExitStack()
    m = MK(nc, st)
    import os

    def chk(name):
        if os.environ.get("KB_STOP") == name:
            m.dead = True
    P = [m.ps([128, 512], F32, "bank%d" % i) for i in range(8)]

    def pb16(k):
        return P[k][:, :].bitcast(BF16)

    for l in range(L):
        for u, (nm, c0, M) in enumerate(UNITS):
            m.dma("pool", w_in_s[l, u, :, :, 0:M], w_in[l][:, c0:c0 + M].rearrange("(c p) m -> p c m", p=128))
        for oc in range(8):
            m.dma("pool", w_out_s[l, oc], w_out[l][:, oc * 128:(oc + 1) * 128].rearrange("(c p) m -> p c m", p=128))
        for j in range(44):
            m.dma("pool", w_up_s[l, j], w_up[l][:, j * 128:(j + 1) * 128].rearrange("(c p) m -> p c m", p=128))
        for j in range(4):
            m.dma("pool", w_down_s[l, j * 704:(j + 1) * 704, :], w_down[l, j * 704:(j + 1) * 704, :])

    chk('preconv')
    def cload(name, shape, dt=F32, src=None, q="sp"):
        t = m.sb(shape, dt, name)
        m.dma("pool" if dt != F32 else q, t[:], cd[name] if src is None else src)
        t.ro = True
        return t

    identF = cload("ident", [128, 128])
    identB = cload("ident", [128, 128], BF16)
    prot = cload("prot", [64, 64])
    caustF = cload("caust", [128, 128])
    cmaskB = cload("cmask", [128, 4], BF16)
    impmul = cload("impmul", [128, 2])
    impadd = cload("impadd", [128, 2])
    efull = cload("efull", [128, 8192], BF16)
    blkB = cload("blk", [128, 4], BF16)
    selh = cload("selh", [4, 4, 128])
    iotaF = cload("iota", [128, 1])
    caustB = m.sb([128, 4, 128], BF16, "caustB")
    causuB = m.sb([128, 4, 128], BF16, "causuB")
    for hh in range(4):
        m.dma("pool", caustB[:, hh, :], cd["caust"])
        m.dma("pool", causuB[:, hh, :], cd["causu"])
    caustB.ro = True
    causuB.ro = True
    onesB = m.sb([128, 128], BF16, "onesB")
    m.memset("dve", onesB[:], 1.0)
    onesB.ro = True
    ones4 = m.sb([4, 128], F32, "ones4")
    m.memset("dve", ones4[:], 1.0)
    ones4.ro = True
    epsT = m.sb([128, 1], F32, "epsT")
    m.memset("dve", epsT[:], EPS)
    epsT.ro = True
    tri01 = m.sb([128, 128], F32, "tri01")
    m.ts("dve", tri01[:], caustF[:], 0.0, ALU.is_equal)
    tri01.ro = True
    ldn = [0]
    ldstg = [m.sb([128, 128], F32, "ldstg%d" % i) for i in range(2)]

    def loadT(dst, src2d, R, ncol=128):
        stg = ldstg[ldn[0] % 2]
        m.dma("sp", stg[0:R, 0:ncol], src2d)
        pk = P[ldn[0] % 2]
        ldn[0] += 1
        m.tr(pk[0:ncol, 0:R], stg[0:R, 0:ncol], identF[0:R, 0:R])
        m.copy("dve", dst, pk[0:ncol, 0:R])

    def storeT(dst2d, src, R, npart=128):
        stg = ldstg[ldn[0] % 2]
        pk = P[ldn[0] % 2]
        ldn[0] += 1
        m.tr(pk[0:R, 0:npart], src, identF[0:npart, 0:npart])
        m.copy("dve", stg[0:R, 0:npart], pk[0:R, 0:npart])
        m.dma("sp", dst2d, stg[0:R, 0:npart])

    fng = m.sb([128, 8], F32, "fng")
    loadT(fng[:], fng_d.rearrange("(c p) -> c p", p=128), 8)
    fng.ro = True
    zero8 = m.sb([128, 8], F32, "zero8")
    m.memset("dve", zero8[:], 0.0)
    zero8.ro = True

    chk('consts')
    MODT = m.sb([128, L, 48, NR], F32, "MODT")
    A1T = m.sb([128, L, 8, NR], F32, "A1T")
    A2T = m.sb([128, L, 8, NR], F32, "A2T")
    with ExitStack() as ph:
        scT = m.sb([128, 8, NR], F32, "scT", ph)
        for c in range(8):
            loadT(scT[:, c, :], cc[:, c * 128:(c + 1) * 128], NR)
        m.act(scT[:], scT[:], AF.Silu)
        abT = m.sb([128, L, 48], F32, "abT", ph)
        for l in range(L):
            loadT(abT[:, l, :], ada_b[l].rearrange("(c p) -> c p", p=128), 48)
        wb = [m.sb([128, 8, 512], F32, "adaw%d" % i, ph) for i in range(2)]
        n = 0
        for l in range(L):
            for ch in range(12):
                w = wb[n % 2]
                n += 1
                m.dma("sp", w[:], ada_w[l][:, ch * 512:(ch + 1) * 512].rearrange("(c p) n -> p c n", p=128))
                for e4 in range(4):
                    ec = ch * 4 + e4
                    pp = P[ec % 2]
                    for dc in range(8):
                        m.mm(pp[:, 0:NR], w[:, dc, e4 * 128:(e4 + 1) * 128], scT[:, dc, :], start=(dc == 0), stop=(dc == 7))
                    m.act(MODT[:, l, ec, :], pp[:, 0:NR], AF.Identity, bias=abT[:, l, ec:ec + 1])
        n1g = m.sb([128, L, 8], F32, "n1g", ph)
        n2g = m.sb([128, L, 8], F32, "n2g", ph)
        for l in range(L):
            loadT(n1g[:, l, :], norm1_g[l].rearrange("(c p) -> c p", p=128), 8)
            loadT(n2g[:, l, :], norm2_g[l].rearrange("(c p) -> c p", p=128), 8)
            for c in range(8):
                m.ts("dve", A1T[:, l, c, :], MODT[:, l, 8 + c, :], 1.0, ALU.add, n1g[:, l, c:c + 1], ALU.mult)
                m.ts("dve", A2T[:, l, c, :], MODT[:, l, 32 + c, :], 1.0, ALU.add, n2g[:, l, c:c + 1], ALU.mult)
        m.barrier()
    MODT.ro = True
    A1T.ro = True
    A2T.ro = True
    m.barrier()
    class NS:
        pass

    def rope_evac(ps, u, LP, cosT, sinT, ncol, xb, t1, pk, outs):
        m.act(xb[:, 0:ncol], ps, AF.Identity, bias=LP.BIN[0:64, u:u + 1])
        m.mm(pk[0:64, 0:ncol], prot[:], xb[:, 0:ncol])
        m.tt("dve", t1[:, 0:ncol], xb[:, 0:ncol], cosT[:, 0:ncol], ALU.mult)
        m.tt("dve", xb[:, 0:ncol], pk[0:64, 0:ncol], sinT[:, 0:ncol], ALU.mult)
        for (e, dst) in outs:
            m.tt(e, dst, t1[:, 0:ncol], xb[:, 0:ncol], ALU.add)

    def proj_unit(l, u, T, pk):
        M = UNITS[u][2]
        w = wload(w_in_s[l, u, :, :, 0:M], M)
        for c in range(8):
            m.mm(pk[0:M, 0:T], w[:, c, 0:M], hT[:, c, 0:T], start=(c == 0), stop=(c == 7))
        return M

    def cmp_branch(LP, g, i, NQ, qa, jq, Kc, Vc_bf, nb, diag, gtok, o_tok, WX, impx, A, NegMT):
        nch = (nb + 127) // 128
        for hh in range(4):
            h = 4 * g + hh
            ps = P[hh % 2]
            m.mm(ps[0:NQ, 0:nb], qa[0:64, jq, hh, :], Kc[0:64, 0:nb], start=True, stop=not diag)
            if diag:
                m.mm(ps[0:NQ, nb - 4:nb], identB[:, 0:NQ], cmaskB[:, :], start=False, stop=True)
            m.reduce(A.mx[hh][0:NQ, :], ps[0:NQ, 0:nb], ALU.max)
            m.ts("dve", A.mx[hh][0:NQ, :], A.mx[hh][0:NQ, :], -1000.0, ALU.max, -0.125, ALU.mult)
            m.act(A.PC[0:NQ, hh, 0:nb], ps[0:NQ, 0:nb], AF.Exp, bias=A.mx[hh][0:NQ, :], scale=0.125, accum_out=A.rs[hh][0:NQ, :])
            m.ts("dve", A.rs[hh][0:NQ, :], A.rs[hh][0:NQ, :], 1e-30, ALU.max)
            m.recip(A.rs[hh][0:NQ, :], A.rs[hh][0:NQ, :])
            m.ts("dve", A.PC[0:NQ, hh, 0:nb], A.PC[0:NQ, hh, 0:nb], A.rs[hh][0:NQ, :], ALU.mult)
            m.ts("dve", A.PG[hh][0:NQ, 0:nb], A.PC[0:NQ, hh, 0:nb], gtok[0:NQ, 3 * h:3 * h + 1], ALU.mult)
            for ch in range(nch):
                w = min(128, nb - ch * 128)
                m.tr(pb16(2)[0:w, ch * 4 * NQ + hh * NQ: ch * 4 * NQ + (hh + 1) * NQ], A.PG[hh][0:NQ, ch * 128:ch * 128 + w], identB[0:NQ, 0:NQ])
        for ch in range(nch):
            w = min(128, nb - ch * 128)
            m.copy("act", A.PTs[0:w, ch, 0:4 * NQ], pb16(2)[0:w, ch * 4 * NQ:(ch + 1) * 4 * NQ])
        for hh in range(4):
            for ch in range(nch):
                w = min(128, nb - ch * 128)
                m.mm(P[3][0:NQ, hh * 64:(hh + 1) * 64], A.PTs[0:w, ch, hh * NQ:(hh + 1) * NQ], Vc_bf[0:w, ch, g, :],
                     start=(ch == 0), stop=(ch == nch - 1))
        m.copy("act", o_tok[0:NQ, 4 * g:4 * g + 4, :], P[3][0:NQ, 0:256].rearrange("p (h d) -> p h d", h=4))
        nsb = nb // 2
        m.reduce(A.imp[0:NQ, 0:nsb], A.PC[0:NQ, :, 0:nb].rearrange("p h (j two) -> p j h two", two=2), ALU.add, axis=AX.XY)
        m.memset("pool", impx[0:NQ, :], -1e30)
        di = 2 * i
        if di > 0:
            m.copy("dve", impx[0:NQ, 0:min(di, nsb)], A.imp[0:NQ, 0:min(di, nsb)])
        if nsb >= di + 2:
            m.tt("dve", impx[0:NQ, di:di + 2], A.imp[0:NQ, di:di + 2], impmul[0:NQ, :], ALU.mult)
            m.tt("dve", impx[0:NQ, di:di + 2], impx[0:NQ, di:di + 2], impadd[0:NQ, :], ALU.add)
        else:
            m.copy("dve", impx[0:NQ, di:di + 2], impadd[0:NQ, :])
        if di > 0:
            m.ts("dve", impx[0:NQ, 0:1], impx[0:NQ, 0:1], 1e4, ALU.add)
        m.op("dve", lambda: nc.vector.max(out=A.m8.t[0:NQ, :], in_=impx.t[0:NQ, :]), [impx], [A.m8])
        m.op("dve", lambda: nc.vector.match_replace(out=A.impx2.t[0:NQ, :], in_to_replace=A.m8.t[0:NQ, :],
                                                     in_values=impx.t[0:NQ, :], imm_value=-1e30), [impx, A.m8], [A.impx2])
        m.op("dve", lambda: nc.vector.max(out=A.m8b.t[0:NQ, :], in_=A.impx2.t[0:NQ, :]), [A.impx2], [A.m8b])
        m.ts("dve", A.thr[0:NQ, :], A.m8b[0:NQ, 7:8], -1e29, ALU.max)
        m.ts("dve", A.negm[0:NQ, :], impx[0:NQ, :], A.thr[0:NQ, :], ALU.is_lt, NEGM, ALU.mult)
        nchs = (WX + 127) // 128
        for ch in range(nchs):
            w = min(128, WX - ch * 128)
            m.tr(pb16(2)[0:w, ch * NQ:(ch + 1) * NQ], A.negm[0:NQ, ch * 128:ch * 128 + w], identB[0:NQ, 0:NQ])
        for ch in range(nchs):
            w = min(128, WX - ch * 128)
            m.copy("act", NegMT[0:w, ch, :, :], pb16(2)[0:w, ch * NQ:(ch + 1) * NQ].unsqueeze(1).bc([w, 4, NQ]))

    def tr_step(NQ, qrhs, ti, last, Kv, Vv, extra, A, oa, sb):
        ps = P[sb + ti % 2] if sb == 4 else P[sb]
        m.mm(ps[:, 0:4 * NQ], Kv, qrhs, start=True, stop=(len(extra) == 0))
        for ei, (lt, rh) in enumerate(extra):
            m.mm(ps[:, 0:4 * NQ], lt, rh, start=False, stop=(ei == len(extra) - 1))
        pt = A.ptr[ti % 2]
        m.act(pt[:, 0:4 * NQ], ps[:, 0:4 * NQ], AF.Exp, scale=0.125)
        m.mm(P[oa][0:65, 0:4 * NQ], Vv, pt[:, 0:4 * NQ], start=(ti == 0), stop=last)

    def tr_finish(g, NQ, br, gtok, o_tok, A, oa):
        m.copy("act", A.OAs[:, 0:4 * NQ], P[oa][0:65, 0:4 * NQ])
        for hh in range(4):
            m.tr(P[7][0:NQ, hh * 65:(hh + 1) * 65], A.OAs[0:65, hh * NQ:(hh + 1) * NQ], identF[0:65, 0:65])
        p7v = P[7][0:NQ, 0:260].rearrange("p (h x) -> p h x", h=4)
        m.recip(A.rden[0:NQ, :], p7v[:, :, 64])
        m.tt("dve", A.rden[0:NQ, :], A.rden[0:NQ, :],
             gtok[0:NQ, 12 * g:12 * g + 12].rearrange("p (h b) -> p h b", b=3)[:, :, br], ALU.mult)
        for hh in range(4):
            m.stt(o_tok[0:NQ, 4 * g + hh, :], P[7][0:NQ, hh * 65:hh * 65 + 64], A.rden[0:NQ, hh:hh + 1],
                  o_tok[0:NQ, 4 * g + hh, :], ALU.mult, ALU.add)

    def tr_score(NQ, qrhs, ti, Kv, extra, sb):
        ps = P[sb + ti % 2] if sb == 4 else P[sb]
        m.mm(ps[:, 0:4 * NQ], Kv, qrhs, start=True, stop=(len(extra) == 0))
        for ei, (lt, rh) in enumerate(extra):
            m.mm(ps[:, 0:4 * NQ], lt, rh, start=False, stop=(ei == len(extra) - 1))
        return ps

    def tr_pv(NQ, ps, ti, last, Vv, A, oa):
        pt = A.ptr[ti % 2]
        m.act(pt[:, 0:4 * NQ], ps[:, 0:4 * NQ], AF.Exp, scale=0.125)
        return pt

    def tr_branch(g, NQ, qrhs, tiles, br, gtok, o_tok, A):
        n = len(tiles)
        ps_next = tr_score(NQ, qrhs, 0, tiles[0][0], tiles[0][2], 4)
        for ti, (Kv, Vv, extra) in enumerate(tiles):
            ps = ps_next
            pt = A.ptr[ti % 2]
            m.act(pt[:, 0:4 * NQ], ps[:, 0:4 * NQ], AF.Exp, scale=0.125)
            if ti + 1 < n:
                ps_next = tr_score(NQ, qrhs, ti + 1, tiles[ti + 1][0], tiles[ti + 1][2], 4)
            m.mm(P[6][0:65, 0:4 * NQ], Vv, pt[:, 0:4 * NQ], start=(ti == 0), stop=(ti == n - 1))
        tr_finish(g, NQ, br, gtok, o_tok, A, 6)

    def attn_alloc(ph, NQW, NBMAX, WX, nneg):
        A = NS()
        A.mx = [m.sb([128, 1], F32, "mx%d" % i, ph) for i in range(4)]
        A.rs = [m.sb([128, 1], F32, "rs%d" % i, ph) for i in range(4)]
        A.PC = m.sb([128, 4, NBMAX], F32, "PC", ph)
        A.PG = [m.sb([128, NBMAX], BF16, "PG%d" % i, ph) for i in range(4)]
        A.PTs = m.sb([128, (NBMAX + 127) // 128, 4 * NQW], BF16, "PTs", ph)
        A.imp = m.sb([128, NBMAX // 2], F32, "imp", ph)
        A.impx2 = m.sb([128, WX], F32, "impx2", ph)
        A.m8 = m.sb([128, 8], F32, "m8", ph)
        A.m8b = m.sb([128, 8], F32, "m8b", ph)
        A.thr = m.sb([128, 1], F32, "thr", ph)
        A.negm = m.sb([128, WX], BF16, "negm", ph)
        A.ptr = [m.sb([128, 4 * NQW], BF16, "ptr%d" % i, ph) for i in range(2)]
        A.OAs = m.sb([65, 4 * NQW], F32, "OAs", ph)
        A.rden = m.sb([128, 4], F32, "rden", ph)
        A.impx = m.sb([128, WX], F32, "impx", ph)
        A.NegMT = [m.sb([128, (WX + 127) // 128, 4, NQW], BF16, "NegMT%d" % i, ph) for i in range(nneg)]
        A.ptr2 = [m.sb([128, 4 * NQW], BF16, "ptrb%d" % i, ph) for i in range(2)] if NQW < 128 else None
        return A
    def assemble_tile(l, LP, T0, ntok, kc, vc, ks, vs, kw, vw, col0, kvr_dst, Vs_dst, Vw_dst, win_dst, A2, pooled_cols):
        R, R2 = P[4], P[5]
        for gg in range(2):
            m.tr(R[0:ntok, gg * 64:(gg + 1) * 64], kc[gg][:, col0:col0 + ntok], identF[0:64, 0:64])
            m.tr(R[0:ntok, 256 + gg * 64:256 + (gg + 1) * 64], ks[gg][:, col0:col0 + ntok], identF[0:64, 0:64])
            m.tr(R2[0:ntok, gg * 64:(gg + 1) * 64], kw[gg][:, col0:col0 + ntok], identF[0:64, 0:64])
        m.tr(R[0:ntok, 128:256], vc[:, col0:col0 + ntok], identF[:])
        m.tr(R[0:ntok, 384:512], vs[:, col0:col0 + ntok], identF[:])
        m.tr(R2[0:ntok, 128:256], vw[:, col0:col0 + ntok], identF[:])
        chk('D1')
        m.copy("dve", A2.rows[0:ntok, :], R[0:ntok, :])
        m.dma("sp", kvr_dst, A2.rows[0:ntok, :])
        chk('D2')
        m.copy("act", Vs_dst, R[0:ntok, 384:512].rearrange("p (g d) -> p g d", g=2))
        m.copy("act", Vw_dst, R2[0:ntok, 128:256].rearrange("p (g d) -> p g d", g=2))
        chk('D3')
        if win_dst is not None:
            m.copy("dve", A2.rows2[0:ntok, :], R2[0:ntok, 0:256])
            m.dma("sp", win_dst, A2.rows2[0:ntok, :])
        chk('D4')
        if pooled_cols is not None:
            m.tt("pool", A2.XA[0:ntok, :], A2.rows[0:ntok, 0:256], LP.Atab[0:ntok, :], ALU.mult)
            for s in range(2):
                m.mm(P[6][:, s * 16 + pooled_cols: s * 16 + pooled_cols + 4], A2.XA[0:ntok, s * 128:(s + 1) * 128], blkB[0:ntok, :])

    def prompt_group(l, gi, LP):
        t0 = gi * 512
        T = 512
        if l == 0:
            with ExitStack() as ph:
                xtok = m.sb([128, 4, D], F32, "xtok", ph)
                m.dma("sp", xtok[:], xp[t0:t0 + 512, :].rearrange("(j p) d -> p j d", p=128))
                for c in range(8):
                    pk = P[c % 2]
                    for j in range(4):
                        m.tr(pk[:, j * 128:(j + 1) * 128], xtok[:, j, c * 128:(c + 1) * 128], identF[:])
                    m.copy("act" if c % 2 else "dve", xT[:, c, :], pk[:, :])
                m.barrier()
        else:
            m.dma("sp", xT[:], xscr[gi])
        chk('A')
        norm_mod(l, A1T, 0, T, [(0, T, 0)], hT)
        chk('B')
        with ExitStack() as ph:
            Qaug = [m.sb([65, 4, 4, 128], BF16, "Qaug%d" % g, ph) for g in range(2)]
            gsig = m.sb([24, 512], F32, "gsig", ph)
            with ExitStack() as ph2:
                cosT = m.sb([64, 512], F32, "cosT", ph2)
                sinT = m.sb([64, 512], F32, "sinT", ph2)
                m.dma("sp", cosT[:], cd["ropecos"][:, t0:t0 + 512])
                m.dma("sp", sinT[:], cd["ropesin"][:, t0:t0 + 512])
                for g in range(2):
                    m.memset("pool", Qaug[g][64:65], 0.0)
                kc = [m.sb([64, 512], F32, "kc%d" % g, ph2) for g in range(2)]
                ks = [m.sb([64, 512], F32, "ks%d" % g, ph2) for g in range(2)]
                kw = [m.sb([64, 512], F32, "kw%d" % g, ph2) for g in range(2)]
                vT = {nm: m.sb([128, 512], F32, nm, ph2) for nm in ("vc", "vs", "vw")}
                xb = [m.sb([64, 512], F32, "xb%d" % i, ph2) for i in range(2)]
                t1 = [m.sb([64, 512], F32, "t1%d" % i, ph2) for i in range(2)]
                n = 0
                for h in range(8):
                    u = UIDX["q%d" % h]
                    pk = P[n % 2]
                    proj_unit(l, u, T, pk)
                    g, hh = h // 4, h % 4
                    rope_evac(pk[0:64, 0:T], u, LP, cosT, sinT, T, xb[n % 2], t1[n % 2], P[2 + n % 2],
                              [("pool", Qaug[g][0:64, :, hh, :])])
                    n += 1
                for bn, kk in (("c", kc), ("s", ks), ("w", kw)):
                    for g in range(2):
                        u = UIDX["k%s%d" % (bn, g)]
                        pk = P[n % 2]
                        proj_unit(l, u, T, pk)
                        outs = [("pool", kk[g][:, :])]
                        rope_evac(pk[0:64, 0:T], u, LP, cosT, sinT, T, xb[n % 2], t1[n % 2], P[2 + n % 2], outs)
                        if bn == "s":
                            m.copy("act", Ksel[g][0:64, t0:t0 + 512], kk[g][:, :])
                        if bn == "w":
                            for j in range(4):
                                rr = ((t0 // 128) + j) % 8
                                m.copy("act", Kwin[g][0:64, rr * 128:(rr + 1) * 128], kk[g][:, j * 128:(j + 1) * 128])
                        n += 1
                    u = UIDX["v" + bn]
                    pk = P[n % 2]
                    proj_unit(l, u, T, pk)
                    m.act(vT["v" + bn][:, :], pk[:, 0:T], AF.Identity, bias=LP.BIN[:, u:u + 1])
                    n += 1
                u = UIDX["gt"]
                pk = P[n % 2]
                proj_unit(l, u, T, pk)
                m.act(gsig[:, :], pk[0:24, 0:T], AF.Sigmoid, bias=LP.BIN[0:24, u:u + 1])
                chk('C1')
                A2 = NS()
                A2.rows = m.sb([128, 512], F32, "rows", ph2)
                A2.rows2 = m.sb([128, 256], F32, "rows2", ph2)
                A2.XA = m.sb([128, 256], BF16, "XA", ph2)
                for j in range(4):
                    ti = gi * 4 + j
                    tok0 = t0 + j * 128
                    wd = None
                    if ti >= NT - 4:
                        wd = win_p[l, (ti - (NT - 4)) * 128:(ti - (NT - 4) + 1) * 128, :]
                    assemble_tile(l, LP, tok0, 128, kc, vT["vc"], ks, vT["vs"], kw, vT["vw"], j * 128,
                                  kvr_p[l, tok0:tok0 + 128, :], Vsel[:, ti, :, 0:64], Vwin[:, ti % 8, :, 0:64], wd, A2, j * 4)
                chk('D5')
                PTp = m.sb([128, 2, 16], BF16, "PTp", ph2)
                m.copy("act", PTp[:], P[6][:, 0:32].rearrange("p (s n) -> p s n", s=2))
                PTpad = m.sb([128, 128], BF16, "PTpad", ph2)
                m.memset("pool", PTpad[:], 0.0)
                blk0 = gi * 16
                cch, coff = blk0 // 128, blk0 % 128
                m.copy("pool", PTpad[:, coff:coff + 16], PTp[:, 1, :])
                for g in range(2):
                    gs = slice(g * 64, (g + 1) * 64)
                    m.mm(P[7][0:64, g * 16:(g + 1) * 16], LP.CW[gs, 0, :], PTp[gs, 0, :])
                    m.copy("act", Kcmp[g][:, blk0:blk0 + 16], P[7][0:64, g * 16:(g + 1) * 16])
                    m.mm(P[7][:, 64 + g * 64:128 + g * 64], PTpad[gs, :], LP.CW[gs, 1, :])
                    m.tt("dve", Vcmp_acc[:, cch, g, :], Vcmp_acc[:, cch, g, :], P[7][:, 64 + g * 64:128 + g * 64], ALU.add)
                    m.copy("pool", Vcmp_bf[:, cch, g, :], Vcmp_acc[:, cch, g, :])
                m.barrier()
            chk('D')
            NBMAX = max(4 * NT, 8)
            A = attn_alloc(ph, 128, NBMAX, 128, 8)
            gtok = [m.sb([128, 24], F32, "gtok%d" % j, ph) for j in range(4)]
            o_tok = [m.sb([128, 8, 64], F32, "o_tok%d" % j, ph) for j in range(4)]
            for j in range(4):
                i = gi * 4 + j
                m.tr(P[7][:, 0:24], gsig[0:24, j * 128:(j + 1) * 128], identF[0:24, 0:24])
                m.copy("dve", gtok[j][:, :], P[7][:, 0:24])
                for g in range(2):
                    nb = 4 * (i + 1)
                    cmp_branch(LP, g, i, 128, Qaug[g], j, Kcmp[g], Vcmp_bf, nb, True, gtok[j], o_tok[j], 128, A.impx, A, A.NegMT[2 * j + g])
            caus = caustB[:, :, :].rearrange("p h q -> p (h q)")
            causu = causuB[:, :, :].rearrange("p h q -> p (h q)")
            for j in range(4):
                i = gi * 4 + j
                for g in range(2):
                    qrhs = Qaug[g][:, j, :, :].rearrange("p h q -> p (h q)")
                    negmt = A.NegMT[2 * j + g][:, 0, :, :].rearrange("p h q -> p (h q)")
                    tiles = []
                    for kt in range(i + 1):
                        extra = [(efull[:, (kt % 64) * 128:(kt % 64 + 1) * 128], negmt)]
                        if kt == i:
                            extra.append((identB[:], caus))
                        tiles.append((Ksel[g][:, kt * 128:(kt + 1) * 128], Vsel[:, kt, g, 0:65], extra))
                    tr_branch(g, 128, qrhs, tiles, 1, gtok[j], o_tok[j], A)
                    tiles = []
                    for kt in range(max(0, i - 4), i + 1):
                        extra = []
                        if kt == i:
                            extra.append((identB[:], caus))
                        if kt == i - 4:
                            extra.append((identB[:], causu))
                        rr = kt % 8
                        tiles.append((Kwin[g][:, rr * 128:(rr + 1) * 128], Vwin[:, rr, g, 0:65], extra))
                    tr_branch(g, 128, qrhs, tiles, 2, gtok[j], o_tok[j], A)
                for c4 in range(4):
                    m.tr(P[3][:, c4 * 128:(c4 + 1) * 128], o_tok[j][:, 2 * c4:2 * c4 + 2, :].rearrange("p h d -> p (h d)"), identF[:])
                m.copy("act", OT[:, 0:4, j * 128:(j + 1) * 128], P[3][:, :].rearrange("p (c q) -> p c q", c=4))
            m.barrier()
    def mlstm_rows(ph, miT, mfT, ncol):
        a = m.sb([4, 512], F32, "ls_a", ph)
        m.act(a[:, 0:ncol], mfT[:, 0:ncol], AF.Abs)
        m.act(a[:, 0:ncol], a[:, 0:ncol], AF.Exp, scale=-1.0)
        m.act(a[:, 0:ncol], a[:, 0:ncol], AF.Ln, bias=ones4[:, 0:1], scale=1.0)
        m.ts("dve", mfT[:, 0:ncol], mfT[:, 0:ncol], 0.0, ALU.min)
        m.tt("dve", mfT[:, 0:ncol], mfT[:, 0:ncol], a[:, 0:ncol], ALU.subtract)

    def mlstm_chunk(LP, S, Lc, c0, QKC, miT, lfT, Vaug, motok, HNdst_cols, M):
        cs = slice(c0, c0 + Lc)
        m.scan(M.bT[:, 0:Lc], ones4[:, 0:Lc], lfT[:, cs], 0.0, ALU.mult, ALU.add)
        m.tt("dve", M.uT[:, 0:Lc], miT[:, cs], M.bT[:, 0:Lc], ALU.subtract)
        m.scan(M.MxT[:, 0:Lc], M.uT[:, 0:Lc], M.uT[:, 0:Lc], S.mstate[:, 0:1], ALU.max, ALU.max)
        m.act(M.wiT[:, 0:Lc], M.MxT[:, 0:Lc], AF.Exp, bias=S.mstate[:, 0:1], scale=-1.0)
        m.tt("dve", M.mtT[:, 0:Lc], M.bT[:, 0:Lc], M.MxT[:, 0:Lc], ALU.add)
        m.act(M.emT[:, 0:Lc], M.mtT[:, 0:Lc], AF.Exp, scale=-1.0)
        m.ts("dve", M.nMx[:, 0:Lc], M.MxT[:, 0:Lc], -1.0, ALU.mult)
        for k3, rw in enumerate((M.uT, M.wiT, M.emT)):
            m.tr(P[7][0:Lc, k3 * 4:(k3 + 1) * 4], rw[0:4, 0:Lc], identF[0:4, 0:4])
        m.copy("dve", M.COLS[0:Lc, :], P[7][0:Lc, 0:12])
        m.copy("dve", S.mstate[:, 0:1], M.mtT[:, Lc - 1:Lc])
        for h in range(4):
            hb, ch = (h % 2) * 64, h // 2
            hs = slice(hb, hb + 64)
            qkc = QKC[hs, ch, cs]
            m.mm(P[0][0:64, 0:Lc], LP.WQ[hs, ch, :], qkc)
            m.mm(P[0][0:64, 128:128 + Lc], LP.WK[hs, ch, :], qkc)
            m.copy("act", M.mqT[:, 0:Lc], P[0][0:64, 0:Lc])
            m.act(M.mkT[:, 0:Lc], P[0][0:64, 128:128 + Lc], AF.Copy, scale=0.125)
            m.mm(P[4][0:Lc, 0:64], qkc, LP.WK[hs, ch, :])
            m.mm(P[1][0:Lc, 0:Lc], M.mkT[:, 0:Lc], M.mqT[:, 0:Lc])
            m.mm(P[1][0:Lc, 128:128 + Lc], selh[0:4, h, 0:Lc], M.nMx[0:4, 0:Lc], start=True, stop=False)
            m.mm(P[1][0:Lc, 128:128 + Lc], identF[0:Lc, 0:Lc], caustF[0:Lc, 0:Lc], start=False, stop=True)
            m.act(M.ET[0:Lc, 0:Lc], P[1][0:Lc, 128:128 + Lc], AF.Exp, bias=M.COLS[0:Lc, h:h + 1])
            m.tt("dve", M.AT[0:Lc, 0:Lc], P[1][0:Lc, 0:Lc], M.ET[0:Lc, 0:Lc], ALU.mult)
            m.mm(P[2][0:Lc, 0:65], M.mqT[:, 0:Lc], S.Caug_bf[:, h, 0:65])
            m.mm(P[3][0:Lc, 0:65], M.AT[0:Lc, 0:Lc], Vaug[0:Lc, h, 0:65])
            m.copy("act", M.E2[0:Lc, :], P[3][0:Lc, 0:65])
            m.stt(M.CB[0:Lc, :], P[2][0:Lc, 0:65], M.COLS[0:Lc, 4 + h:5 + h], M.E2[0:Lc, :], ALU.mult, ALU.add)
            m.act(M.DN[0:Lc, :], M.CB[0:Lc, 64:65], AF.Abs)
            m.ts("dve", M.DN[0:Lc, :], M.DN[0:Lc, :], M.COLS[0:Lc, 8 + h:9 + h], ALU.max)
            m.recip(M.DN[0:Lc, :], M.DN[0:Lc, :])
            m.ts("dve", M.HT[0:Lc, h, :], M.CB[0:Lc, 0:64], M.DN[0:Lc, :], ALU.mult)
            m.ts("dve", M.WKs[0:Lc, :], P[4][0:Lc, 0:64], M.ET[0:Lc, Lc - 1:Lc], ALU.mult, 0.125, ALU.mult)
            m.mm(P[5][0:64, 0:65], M.WKs[0:Lc, :], Vaug[0:Lc, h, 0:65])
            m.mm(P[5][0:64, 128:129], selh[0:4, h, 0:64], M.wiT[0:4, Lc - 1:Lc])
            m.copy("act", M.DEC[:, :], P[5][0:64, 128:129])
            m.stt(S.Caug[:, h, :], S.Caug[:, h, :], M.DEC[:, 0:1], P[5][0:64, 0:65], ALU.mult, ALU.add)
            m.copy("pool", S.Caug_bf[:, h, 0:65], S.Caug[:, h, :])
        m.tt("pool", M.SQ[0:Lc, :, :], M.HT[0:Lc, :, :], M.HT[0:Lc, :, :], ALU.mult)
        m.reduce(M.ss[0:Lc, :], M.SQ[0:Lc, :, :], ALU.add)
        m.act(M.ss[0:Lc, :], M.ss[0:Lc, :], AF.Sqrt, bias=epsT[0:Lc, :], scale=1.0 / 64)
        m.recip(M.ss[0:Lc, :], M.ss[0:Lc, :])
        m.tt("dve", M.HT[0:Lc, :, :], M.HT[0:Lc, :, :], M.ss[0:Lc, :].unsqueeze(2).bc([Lc, 4, 64]), ALU.mult)
        hflat = M.HT[0:Lc, :, :].rearrange("p h d -> p (h d)")
        m.tt("dve", hflat, hflat, LP.mng[0:Lc, :], ALU.mult)
        m.tt("dve", hflat, hflat, motok, ALU.mult)

    def mlstm_alloc(ph):
        M = NS()
        for nm in ("bT", "uT", "MxT", "wiT", "mtT", "emT", "nMx"):
            setattr(M, nm, m.sb([4, 128], F32, nm, ph))
        M.COLS = m.sb([128, 12], F32, "COLS", ph)
        M.mqT = m.sb([64, 128], BF16, "mqT", ph)
        M.mkT = m.sb([64, 128], BF16, "mkT", ph)
        M.ET = m.sb([128, 128], F32, "ET", ph)
        M.AT = m.sb([128, 128], BF16, "AT", ph)
        M.E2 = m.sb([128, 65], F32, "E2", ph)
        M.CB = m.sb([128, 65], F32, "CB", ph)
        M.DN = m.sb([128, 1], F32, "DN", ph)
        M.HT = m.sb([128, 4, 64], F32, "HT", ph)
        M.SQ = m.sb([128, 4, 64], F32, "SQ", ph)
        M.ss = m.sb([128, 4], F32, "ss", ph)
        M.WKs = m.sb([128, 64], BF16, "WKs", ph)
        M.DEC = m.sb([64, 1], F32, "DEC", ph)
        return M

    def gelu_tanh(e2, dst, src, tmp):
        m.tt("pool", tmp, src, src, ALU.mult)
        m.ts("pool", tmp, tmp, 0.044715, ALU.mult, 1.0, ALU.add)
        m.tt("pool", tmp, tmp, src, ALU.mult)
        m.act(tmp, tmp, AF.Sigmoid, scale=1.5957691216057308)
        m.tt(e2, dst, tmp, src, ALU.mult)

    def prompt_group2(l, gi, LP, S):
        t0 = gi * 512
        T = 512
        chk('E')
        with ExitStack() as ph:
            mvT = m.sb([128, 2, 512], F32, "mvT", ph)
            moT = m.sb([128, 2, 512], F32, "moT", ph)
            miT = m.sb([4, 512], F32, "miT", ph)
            mfT = m.sb([4, 512], F32, "mfT", ph)
            QKC = m.sb([128, 2, 512], BF16, "QKC", ph)
            acc = m.sb([128, 512], F32, "mc_acc", ph)
            n = 0
            for ch in range(2):
                u = UIDX["mqk%d" % ch]
                pk = P[n % 2]
                n += 1
                proj_unit(l, u, T, pk)
                m.act(LP.MQ[:, ch, 3:515], pk[:, 0:T], AF.Identity, bias=LP.BIN[:, u:u + 1])
                m.act(acc[:, :], LP.MQ[:, ch, 3:515], AF.Identity, bias=LP.mcb[:, ch:ch + 1], scale=LP.mcw[:, 3, ch:ch + 1])
                for jt in range(3):
                    m.stt(acc[:, :], LP.MQ[:, ch, jt:jt + 512], LP.mcw[:, jt, ch:ch + 1], acc[:, :], ALU.mult, ALU.add)
                m.act(QKC[:, ch, :], acc[:, :], AF.Silu)
                if gi == NG - 1:
                    storeT(mconv_p[l][:, ch * 128:(ch + 1) * 128], LP.MQ[:, ch, 512:515], 3)
                m.copy("dve", LP.MQ[:, ch, 0:3], LP.MQ[:, ch, 512:515])
            for nm, dst, fn in (("mv", mvT, AF.Identity), ("mo", moT, AF.Sigmoid)):
                for ch in range(2):
                    u = UIDX["%s%d" % (nm, ch)]
                    pk = P[n % 2]
                    n += 1
                    proj_unit(l, u, T, pk)
                    m.act(dst[:, ch, :], pk[:, 0:T], fn, bias=LP.BIN[:, u:u + 1])
            for nm, dst in (("mi", miT), ("mf", mfT)):
                u = UIDX[nm]
                pk = P[n % 2]
                n += 1
                proj_unit(l, u, T, pk)
                m.act(dst[:, :], pk[0:4, 0:T], AF.Identity, bias=LP.BIN[0:4, u:u + 1])
            mlstm_rows(ph, miT, mfT, T)
            M = mlstm_alloc(ph)
            Vaug = m.sb([128, 4, 66], BF16, "Vaug", ph)
            m.memset("pool", Vaug[:], 1.0)
            motok = m.sb([128, 256], F32, "motok", ph)
            for j in range(4):
                cs = slice(j * 128, (j + 1) * 128)
                for ch in range(2):
                    m.tr(P[6][:, ch * 128:(ch + 1) * 128], mvT[:, ch, cs], identF[:])
                    m.tr(P[6][:, 256 + ch * 128:256 + (ch + 1) * 128], moT[:, ch, cs], identF[:])
                m.copy("act", Vaug[:, :, 0:64], P[6][:, 0:256].rearrange("p (h d) -> p h d", h=4))
                m.copy("dve", motok[:, :], P[6][:, 256:512])
                mlstm_chunk(LP, S, 128, j * 128, QKC, miT, mfT, Vaug, motok[:, :], None, M)
                for ch in range(2):
                    m.tr(P[6][:, ch * 128:(ch + 1) * 128], M.HT[:, 2 * ch:2 * ch + 2, :].rearrange("p h d -> p (h d)"), identF[:])
                m.copy("act", OT[:, 4:6, cs], P[6][:, 0:256].rearrange("p (c q) -> p c q", c=2))
            m.barrier()
        chk('F')
        with ExitStack() as ph:
            cuT = m.sb([128, 2, 512], F32, "cuT", ph)
            cvT = m.sb([128, 2, 512], F32, "cvT", ph)
            tmpg = m.sb([128, 512], F32, "tmpg", ph)
            n = 0
            for nm, dst in (("cu", cuT), ("cv", cvT)):
                for ch in range(2):
                    u = UIDX["%s%d" % (nm, ch)]
                    pk = P[n % 2]
                    n += 1
                    proj_unit(l, u, T, pk)
                    m.act(dst[:, ch, :], pk[:, 0:T], AF.Identity, bias=LP.BIN[:, u:u + 1])
                    gelu_tanh("pool", dst[:, ch, :], dst[:, ch, :], tmpg[:, :])
            sqv = m.sb([128, 2, 512], BF16, "sqv", ph)
            m.act(sqv[:], cvT[:], AF.Square)
            for ch in range(2):
                m.mm(P[2][:, 0:T], onesB[:], sqv[:, ch, :], start=(ch == 0), stop=(ch == 1))
            rstd = m.sb([128, 512], F32, "rstdv", ph)
            m.act(rstd[:], P[2][:, 0:T], AF.Sqrt, bias=epsT[:], scale=1.0 / 256)
            m.recip(rstd[:], rstd[:])
            for ch in range(2):
                m.stt(cvT[:, ch, :], cvT[:, ch, :], LP.cng[:, ch:ch + 1], rstd[:], ALU.mult, ALU.mult)
            VPA = m.sb([128, 2, 128], BF16, "VPA", ph)
            VPB = m.sb([128, 2, 128], BF16, "VPB", ph)
            m.memset("pool", VPA[:], 0.0)
            m.memset("pool", VPB[:], 0.0)
            tmpc = m.sb([128, 128], F32, "tmpc", ph)
            for j in range(4):
                cs = slice(j * 128, (j + 1) * 128)
                for ch in range(2):
                    m.tr(P[3][:, ch * 128:(ch + 1) * 128], cvT[:, ch, cs], identF[:])
                m.copy("act", VPA[:, :, 0:64], P[3][:, 0:256].rearrange("p (c x) -> p c x", c=2)[:, :, 0:64])
                m.copy("act", VPB[:, :, 64:128], P[3][:, 0:256].rearrange("p (c x) -> p c x", c=2)[:, :, 64:128])
                for ch in range(2):
                    pk = P[ch]
                    m.mm(pk[:, 0:128], VPA[:, ch, :], LP.WsT[:, 2 * ch, :], start=True, stop=False)
                    m.mm(pk[:, 0:128], VPB[:, ch, :], LP.WsT[:, 2 * ch + 1, :], start=False, stop=True)
                    m.tt("dve", tmpc[:, :], pk[:, 0:128], LP.BS[:, ch, :], ALU.add)
                    m.tt("dve", OT[:, 6 + ch, cs], tmpc[:, :], cuT[:, ch, cs], ALU.mult)
            m.barrier()
        chk('G')
        for oc in range(8):
            w = wload(w_out_s[l, oc])
            pk = P[oc % 2]
            for c in range(8):
                m.mm(pk[:, 0:T], w[:, c, :], OT[:, c, 0:T], start=(c == 0), stop=(c == 7))
            m.stt(xT[:, oc, 0:T], pk[:, 0:T], MODT[:, l, 16 + oc, 0:1], xT[:, oc, 0:T], ALU.mult, ALU.add)
        chk('H')
        norm_mod(l, A2T, 3, T, [(0, T, 0)], hT)
        ffn(l, LP, T, [(0, T, 0)], 1, 512)
        if gi == NG - 1:
            for r_ in range(2):
                storeT(fconv_p[l, r_].rearrange("(c p) -> c p", p=128), LP.FT[:, :, r_], 44)
        chk('I')
        if l < L - 1:
            m.dma("sp", xscr[gi], xT[:])
        else:
            final_out(T, y_p, t0)

    def ffn(l, LP, T, segs, nseq, seqlen):
        with ExitStack() as ph:
            ACTT = m.sb([128, 22, 512], BF16, "ACTT", ph)
            U = [[m.sb([128, nseq, seqlen + 2], F32, "U%d%d" % (a, b), ph) for b in range(2)] for a in range(2)]
            Y = [[m.sb([128, nseq, seqlen], F32, "Y%d%d" % (a, b), ph) for b in range(2)] for a in range(2)]
            for jf in range(22):
                pr = jf % 2
                for ab in range(2):
                    fc = jf + 22 * ab
                    w = wload(w_up_s[l, fc])
                    pk = P[2 * pr + ab]
                    for c in range(8):
                        m.mm(pk[:, 0:T], w[:, c, :], hT[:, c, 0:T], start=(c == 0), stop=(c == 7))
                    u_, y_ = U[pr][ab], Y[pr][ab]
                    m.copy("pool", u_[:, :, 0:2], LP.FT[:, fc, :].unsqueeze(1).bc([128, nseq, 2]) if nseq == 1 else LP.FTs[:, fc, :, :])
                    m.copy("act", u_[:, :, 2:seqlen + 2], pk[:, 0:T].rearrange("p (s t) -> p s t", s=nseq))
                    m.act(y_[:, :, :], u_[:, :, 2:seqlen + 2], AF.Identity, bias=LP.fcb[:, fc:fc + 1], scale=LP.fcw[:, 2, fc:fc + 1])
                    m.stt(y_[:, :, :], u_[:, :, 1:seqlen + 1], LP.fcw[:, 1, fc:fc + 1], y_[:, :, :], ALU.mult, ALU.add)
                    m.stt(y_[:, :, :], u_[:, :, 0:seqlen], LP.fcw[:, 0, fc:fc + 1], y_[:, :, :], ALU.mult, ALU.add)
                    if nseq == 1:
                        m.copy("pool", LP.FT[:, fc, :], u_[:, 0, seqlen:seqlen + 2])
                    else:
                        m.copy("pool", LP.FTs[:, fc, :, :], u_[:, :, seqlen:seqlen + 2])
                ya, yb = Y[pr][0], Y[pr][1]
                m.act(ya[:, :, :], ya[:, :, :], AF.Silu)
                m.tt("pool", ACTT[:, jf, 0:T].rearrange("p (s t) -> p s t", s=nseq), ya[:, :, :], yb[:, :, :], ALU.mult)
            wd = [m.sb([128, 512], BF16, "wd%d" % i, ph) for i in range(3)]
            nw = 0
            for half in range(2):
                for jf in range(22):
                    w = wd[nw % 3]
                    nw += 1
                    m.dma("sp", w[:], w_down_s[l, jf * 128:(jf + 1) * 128, half * 512:(half + 1) * 512])
                    for o4 in range(4):
                        m.mm(P[4 + o4][:, 0:T], w[:, o4 * 128:(o4 + 1) * 128], ACTT[:, jf, 0:T], start=(jf == 0), stop=(jf == 21))
                for o4 in range(4):
                    oc = half * 4 + o4
                    for (c0, n, r) in segs:
                        m.stt(xT[:, oc, c0:c0 + n], P[4 + o4][:, c0:c0 + n], MODT[:, l, 40 + oc, r:r + 1], xT[:, oc, c0:c0 + n],
                              ALU.mult, ALU.add)
            m.barrier()

    def final_out(T, ydst, tok0):
        with ExitStack() as ph:
            yT = m.sb([128, 8, 512], F32, "yT", ph)
            norm_mod(0, fng, None, T, [(0, T, 0)], yT)
            ytok = [m.sb([128, D], F32, "ytok%d" % i, ph) for i in range(2)]
            ntile = (T + 127) // 128
            for j in range(ntile):
                nt_ = min(128, T - j * 128)
                yt = ytok[j % 2]
                for half in range(2):
                    pk = P[half]
                    for c4 in range(4):
                        c = half * 4 + c4
                        m.tr(pk[0:nt_, c4 * 128:(c4 + 1) * 128], yT[:, c, j * 128:j * 128 + nt_], identF[:])
                    m.copy("act" if half else "dve", yt[0:nt_, half * 512:(half + 1) * 512], pk[0:nt_, :])
                m.dma("sp", ydst[tok0 + j * 128:tok0 + j * 128 + nt_, :], yt[0:nt_, :])
            m.barrier()
    def sample_layer(l, LP):
        T = NTOK_S
        NQ = 8
        PAST = cfg.PAST
        NPG = cfg.NPG
        i_t = PAST // 128
        nbc = PAST // 32
        segs = [(8 * s, 8, 1 + s) for s in range(NSS)]
        sst = ExitStack()
        if l == 0:
            xtk = m.sb([T, D], F32, "xtk", sst)
            m.dma("sp", xtk[:], xs)
            for c in range(8):
                m.tr(P[c % 2][:, 0:T], xtk[:, c * 128:(c + 1) * 128], identF[0:T, 0:T])
                m.copy("dve", xT[:, c, 0:T], P[c % 2][:, 0:T])
        else:
            m.dma("sp", xT[:, :, 0:T], xscr_s)
        norm_mod(l, A1T, 0, T, segs, hT)
        cosS = m.sb([64, T], F32, "cosS", sst)
        sinS = m.sb([64, T], F32, "sinS", sst)
        for s in range(NSS):
            m.dma("sp", cosS[:, 8 * s:8 * s + 8], cd["ropecos"][:, SEQ:SEQ + 8])
            m.dma("sp", sinS[:, 8 * s:8 * s + 8], cd["ropesin"][:, SEQ:SEQ + 8])
        Qs = [m.sb([65, NSS, 4, NQ], BF16, "Qs%d" % g, sst) for g in range(2)]
        for g in range(2):
            m.memset("pool", Qs[g][64:65], 0.0)
        kc = [m.sb([64, T], F32, "skc%d" % g, sst) for g in range(2)]
        ks = [m.sb([64, T], F32, "sks%d" % g, sst) for g in range(2)]
        kw = [m.sb([64, T], F32, "skw%d" % g, sst) for g in range(2)]
        vT = {nm: m.sb([128, T], F32, "s" + nm, sst) for nm in ("vc", "vs", "vw")}
        gsig = m.sb([24, T], F32, "sgsig", sst)
        xb = [m.sb([64, T], F32, "sxb%d" % i, sst) for i in range(2)]
        t1 = [m.sb([64, T], F32, "st1%d" % i, sst) for i in range(2)]
        n = 0
        for h in range(8):
            u = UIDX["q%d" % h]
            pk = P[n % 2]
            proj_unit(l, u, T, pk)
            g, hh = h // 4, h % 4
            rope_evac(pk[0:64, 0:T], u, LP, cosS, sinS, T, xb[n % 2], t1[n % 2], P[2 + n % 2], [("pool", Qs[g][0:64, :, hh, :])])
            n += 1
        for bn, kk in (("c", kc), ("s", ks), ("w", kw)):
            for g in range(2):
                u = UIDX["k%s%d" % (bn, g)]
                pk = P[n % 2]
                proj_unit(l, u, T, pk)
                rope_evac(pk[0:64, 0:T], u, LP, cosS, sinS, T, xb[n % 2], t1[n % 2], P[2 + n % 2], [("pool", kk[g][:, :])])
                n += 1
            u = UIDX["v" + bn]
            pk = P[n % 2]
            proj_unit(l, u, T, pk)
            m.act(vT["v" + bn][:, :], pk[:, 0:T], AF.Identity, bias=LP.BIN[:, u:u + 1])
            n += 1
        u = UIDX["gt"]
        pk = P[n % 2]
        proj_unit(l, u, T, pk)
        m.act(gsig[:, :], pk[0:24, 0:T], AF.Sigmoid, bias=LP.BIN[0:24, u:u + 1])
        pti = m.sb([128, NSS * NPG], I32, "pti", sst)
        m.dma("sp", pti[:], ptab.rearrange("s p -> (s p)").partition_broadcast(128))
        IDX = m.sb([128, NSS * NPG], I32, "IDX", sst)
        m.ts("dve", IDX[:], pti[:], 128.0, ALU.mult, iotaF[:, 0:1], ALU.add)
        if l > 0:
            m.ts("dve", IDX[:], IDX[:], float(l * cfg.NPOOL * 128), ALU.add)
        cache_flat = cache.rearrange("l n c -> (l n) c")
        IDX.ro = True
        WX = 2 * i_t + 2
        A = attn_alloc(sst, NQ, nbc, WX, 2)
        for g in range(2):
            m.memset("pool", A.NegMT[g][:], 0.0)
        NCHc = (nbc + 127) // 128
        Kcs = [m.sb([64, NCHc * 128], BF16, "Kcs%d" % g, sst) for g in range(2)]
        Vcs = m.sb([128, NCHc, 2, 64], BF16, "Vcs", sst)
        A2 = NS()
        A2.rows = m.sb([128, 512], F32, "srows", sst)
        A2.rows2 = m.sb([128, 256], F32, "srows2", sst)
        A2.XA = m.sb([128, 256], BF16, "sXA", sst)
        RP = [m.sb([128, 512], F32, "RP%d" % i, sst) for i in range(6)]
        XAs = [m.sb([128, 256], BF16, "XAs%d" % i, sst) for i in range(2)]
        PTs_ = m.sb([128, 2, 128], BF16, "PTs_", sst)
        Kt = [[m.sb([65, 128], BF16, "Kt%d%d" % (g, i), sst) for i in range(3)] for g in range(2)]
        Vt = [m.sb([128, 2, 66], BF16, "Vt%d" % i, sst) for i in range(3)]
        for g in range(2):
            for i in range(3):
                m.memset("pool", Kt[g][i][:], 1.0)
        for i in range(3):
            m.memset("pool", Vt[i][:], 1.0)
        Ktn = {bn: [m.sb([65, 128], BF16, "Ktn%s%d" % (bn, g), sst) for g in range(2)] for bn in ("s", "w")}
        Vtn = {bn: m.sb([128, 2, 66], BF16, "Vtn" + bn, sst) for bn in ("s", "w")}
        gtok = m.sb([128, 24], F32, "sgtok", sst)
        o_tok = m.sb([128, 8, 64], F32, "so_tok", sst)
        caus8t = m.sb([128, 4, NQ], BF16, "caus8", sst)
        causu8t = m.sb([128, 4, NQ], BF16, "causu8", sst)
        m.copy("pool", caus8t[:], caustB[:, :, 0:NQ])
        m.copy("pool", causu8t[:], causuB[:, :, 0:NQ])
        caus8 = caus8t[:, :, :]
        causu8 = causu8t[:, :, :]
        rpn = [0]

        def kv_tile(src_ap, idx_col, ti):
            rp = RP[rpn[0] % 6]
            rpn[0] += 1
            if idx_col is None:
                m.dma("sp", rp[:, 256:512], src_ap)
            else:
                m.idma(rp[:], cache_flat, IDX[:, idx_col:idx_col + 1])
            b = ti % 3
            for g in range(2):
                m.tr(P[2][0:64, g * 128:(g + 1) * 128], rp[:, 256 + g * 64:256 + (g + 1) * 64], identF[:])
                m.copy("act", Kt[g][b][0:64, :], P[2][0:64, g * 128:(g + 1) * 128])
            m.copy("dve", Vt[b][:, :, 0:64], rp[:, 384:512].rearrange("p (g d) -> p g d", g=2))
            return [Kt[0][b], Kt[1][b]], Vt[b]

        for s in range(NSS):
            cs = slice(8 * s, 8 * s + 8)
            for bn in ("s", "w"):
                for g in range(2):
                    m.memset("pool", Ktn[bn][g][0:64, :], 0.0)
                    m.memset("pool", Ktn[bn][g][64:65, :], 1.0)
                m.memset("pool", Vtn[bn][:], 0.0)
                m.memset("pool", Vtn[bn][:, :, 64:65], 1.0)
            for g in range(2):
                m.copy("act", Ktn["s"][g][0:64, 0:8], ks[g][:, cs])
                m.copy("act", Ktn["w"][g][0:64, 0:8], kw[g][:, cs])
            assemble_tile(l, LP, 0, 8, kc, vT["vc"], ks, vT["vs"], kw, vT["vw"], 8 * s,
                          kvr_s[l, 8 * s:8 * s + 8, :], Vtn["s"][0:8, :, 0:64], Vtn["w"][0:8, :, 0:64],
                          win_s[l, s, 504:512, :], A2, None)
            m.dma("sp", win_s[l, s, 0:504, :], wkv[l, s, 8:512, :])
            m.tr(P[7][0:8, 0:24], gsig[0:24, cs], identF[0:24, 0:24])
            m.copy("dve", gtok[0:8, :], P[7][0:8, 0:24])
            for pg in range(NPG):
                rp = RP[rpn[0] % 6]
                rpn[0] += 1
                m.idma(rp[:], cache_flat, IDX[:, s * NPG + pg:s * NPG + pg + 1])
                xa = XAs[pg % 2]
                m.tt("pool", xa[:], rp[:, 0:256], LP.Atab[:], ALU.mult)
                pc = pg % 32
                for s2 in range(2):
                    m.mm(P[6][:, s2 * 128 + pc * 4:s2 * 128 + pc * 4 + 4], xa[:, s2 * 128:(s2 + 1) * 128], blkB[:, :])
                if pc == 31 or pg == NPG - 1:
                    nbk = (pc + 1) * 4
                    chn = pg // 32
                    m.copy("act", PTs_[:, :, 0:nbk], P[6][:, 0:256].rearrange("p (s n) -> p s n", s=2)[:, :, 0:nbk])
                    for g in range(2):
                        gs = slice(g * 64, (g + 1) * 64)
                        m.mm(P[7][0:64, g * 128:g * 128 + nbk], LP.CW[gs, 0, :], PTs_[gs, 0, 0:nbk])
                        m.copy("act", Kcs[g][:, chn * 128:chn * 128 + nbk], P[7][0:64, g * 128:g * 128 + nbk])
                        m.mm(P[7][0:nbk, 256 + g * 64:320 + g * 64], PTs_[gs, 1, 0:nbk], LP.CW[gs, 1, :])
                        m.copy("act", Vcs[0:nbk, chn, g, :], P[7][0:nbk, 256 + g * 64:320 + g * 64])
            for g in range(2):
                cmp_branch(LP, g, i_t, NQ, Qs[g], s, Kcs[g], Vcs, nbc, False, gtok, o_tok, WX, A.impx, A, A.NegMT[g])
            qr = [Qs[g][:, s, :, :].rearrange("p h q -> p (h q)") for g in range(2)]
            sbank = [[P[4], P[5]], [P[0], P[1]]]
            ptrs = [A.ptr, A.ptr2]
            oab = [6, 3]

            def run_stream(ntile, get_tile, get_extra, br):
                pend = None
                for ti in range(ntile):
                    last = ti == ntile - 1
                    Kv, Vv = get_tile(ti, last)
                    cur = []
                    for g in range(2):
                        ps = sbank[g][ti % 2]
                        extra = get_extra(ti, last, g)
                        m.mm(ps[:, 0:4 * NQ], Kv[g][:, :], qr[g], start=True, stop=(len(extra) == 0))
                        for ei, (lt, rh) in enumerate(extra):
                            m.mm(ps[:, 0:4 * NQ], lt, rh, start=False, stop=(ei == len(extra) - 1))
                        pt = ptrs[g][ti % 2]
                        m.act(pt[:, 0:4 * NQ], ps[:, 0:4 * NQ], AF.Exp, scale=0.125)
                        cur.append((pt, Vv, ti))
                    if pend is not None:
                        for g in range(2):
                            pt, Vp, tp_ = pend[g]
                            m.mm(P[oab[g]][0:65, 0:4 * NQ], Vp[:, g, 0:65], pt[:, 0:4 * NQ], start=(tp_ == 0), stop=False)
                    pend = cur
                for g in range(2):
                    pt, Vp, tp_ = pend[g]
                    m.mm(P[oab[g]][0:65, 0:4 * NQ], Vp[:, g, 0:65], pt[:, 0:4 * NQ], start=(tp_ == 0), stop=True)
                for g in range(2):
                    tr_finish(g, NQ, br, gtok, o_tok, A, oab[g])

            def sel_tile(ti, last):
                if last:
                    return Ktn["s"], Vtn["s"]
                return kv_tile(None, s * NPG + ti, ti)

            def sel_extra(ti, last, g):
                ch = (2 * ti) // 128
                w = min(128, WX - ch * 128)
                ex = [(efull[0:w, (ti % 64) * 128:(ti % 64 + 1) * 128], A.NegMT[g][0:w, ch, :, :].rearrange("p h q -> p (h q)"))]
                if last:
                    ex.append((identB[:], caus8.rearrange("p h q -> p (h q)")))
                return ex

            run_stream(NPG + 1, sel_tile, sel_extra, 1)

            def win_tile(ti, last):
                if last:
                    return Ktn["w"], Vtn["w"]
                return kv_tile(wkv[l, s, ti * 128:(ti + 1) * 128, :], None, ti)

            def win_extra(ti, last, g):
                ex = []
                if ti == 0:
                    ex.append((identB[:], causu8.rearrange("p h q -> p (h q)")))
                if last:
                    ex.append((identB[:], caus8.rearrange("p h q -> p (h q)")))
                return ex

            run_stream(5, win_tile, win_extra, 2)
            for c4 in range(4):
                m.tr(P[2][:, c4 * NQ:(c4 + 1) * NQ], o_tok[0:NQ, 2 * c4:2 * c4 + 2, :].rearrange("p h d -> p (h d)"), identF[0:NQ, 0:NQ])
            m.copy("act", OT[:, 0:4, cs], P[2][:, 0:4 * NQ].rearrange("p (c q) -> p c q", c=4))
        m.barrier()
        sst.close()
        sample_layer2(l, LP, T, segs)

    def sample_layer2(l, LP, T, segs):
        NQ = 8
        with ExitStack() as ph:
            mvT = m.sb([128, 2, T], F32, "smvT", ph)
            moT = m.sb([128, 2, T], F32, "smoT", ph)
            miT = m.sb([4, 512], F32, "smiT", ph)
            mfT = m.sb([4, 512], F32, "smfT", ph)
            QKC = m.sb([128, 2, T], BF16, "sQKC", ph)
            MQs = m.sb([128, 2, NSS, 11], F32, "MQs", ph)
            acc = m.sb([128, NSS, 8], F32, "smc_acc", ph)
            for s in range(NSS):
                for ch in range(2):
                    loadT(MQs[:, ch, s, 0:3], sconv[l, s][:, ch * 128:(ch + 1) * 128], 3)
            n = 0
            for ch in range(2):
                u = UIDX["mqk%d" % ch]
                pk = P[2 + n % 2]
                n += 1
                proj_unit(l, u, T, pk)
                m.act(MQs[:, ch, :, 3:11], pk[:, 0:T].rearrange("p (s t) -> p s t", s=NSS), AF.Identity, bias=LP.BIN[:, u:u + 1])
                m.act(acc[:, :, :], MQs[:, ch, :, 3:11], AF.Identity, bias=LP.mcb[:, ch:ch + 1], scale=LP.mcw[:, 3, ch:ch + 1])
                for jt in range(3):
                    m.stt(acc[:, :, :], MQs[:, ch, :, jt:jt + 8], LP.mcw[:, jt, ch:ch + 1], acc[:, :, :], ALU.mult, ALU.add)
                m.act(QKC[:, ch, :].rearrange("p (s t) -> p s t", s=NSS), acc[:, :, :], AF.Silu)
                for s in range(NSS):
                    storeT(mconv_s[l, s][:, ch * 128:(ch + 1) * 128], MQs[:, ch, s, 8:11], 3)
            for nm, dst, fn in (("mv", mvT, AF.Identity), ("mo", moT, AF.Sigmoid)):
                for ch in range(2):
                    u = UIDX["%s%d" % (nm, ch)]
                    pk = P[2 + n % 2]
                    n += 1
                    proj_unit(l, u, T, pk)
                    m.act(dst[:, ch, :], pk[:, 0:T], fn, bias=LP.BIN[:, u:u + 1])
            for nm, dst in (("mi", miT), ("mf", mfT)):
                u = UIDX[nm]
                pk = P[2 + n % 2]
                n += 1
                proj_unit(l, u, T, pk)
                m.act(dst[:, 0:T], pk[0:4, 0:T], AF.Identity, bias=LP.BIN[0:4, u:u + 1])
            mlstm_rows(ph, miT, mfT, T)
            M = mlstm_alloc(ph)
            Vaug = m.sb([128, 4, 66], BF16, "sVaug", ph)
            m.memset("pool", Vaug[:], 1.0)
            motok = m.sb([128, 256], F32, "smotok", ph)
            S = NS()
            S.Caug = m.sb([64, 4, 65], F32, "sCaug", ph)
            S.Caug_bf = m.sb([64, 4, 66], BF16, "sCaug_bf", ph)
            S.mstate = m.sb([4, 1], F32, "smstate", ph)
            for s in range(NSS):
                cs = slice(8 * s, 8 * s + 8)
                for h in range(4):
                    loadT(S.Caug[:, h, 0:64], sC[l, s, h], 64, 64)
                    loadT(S.Caug[:, h, 64:65], sn[l, s, h:h + 1, :], 1, 64)
                    m.copy("pool", S.Caug_bf[:, h, 0:65], S.Caug[:, h, :])
                m.dma("sp", S.mstate[:, 0:1], smm[l, s].rearrange("(h o) -> h o", o=1), allow_slow_non_contiguous=True)
                for ch in range(2):
                    m.tr(P[6][0:8, ch * 128:(ch + 1) * 128], mvT[:, ch, cs], identF[:])
                    m.tr(P[6][0:8, 256 + ch * 128:256 + (ch + 1) * 128], moT[:, ch, cs], identF[:])
                m.copy("act", Vaug[0:8, :, 0:64], P[6][0:8, 0:256].rearrange("p (h d) -> p h d", h=4))
                m.copy("dve", motok[0:8, :], P[6][0:8, 256:512])
                mlstm_chunk(LP, S, 8, 8 * s, QKC, miT, mfT, Vaug, motok[0:8, :], None, M)
                for ch in range(2):
                    m.tr(P[6][:, ch * 8:(ch + 1) * 8], M.HT[0:8, 2 * ch:2 * ch + 2, :].rearrange("p h d -> p (h d)"), identF[0:8, 0:8])
                m.copy("act", OT[:, 4:6, cs], P[6][:, 0:16].rearrange("p (c q) -> p c q", c=2))
                for h in range(4):
                    storeT(mC_s[l, s, h], S.Caug[:, h, 0:64], 64, 64)
                    storeT(mn_s[l, s, h:h + 1, :], S.Caug[:, h, 64:65], 1, 64)
                m.dma("sp", mm_s[l, s].rearrange("(h o) -> h o", o=1), S.mstate[:, 0:1], allow_slow_non_contiguous=True)
            m.barrier()
        with ExitStack() as ph:
            cuT = m.sb([128, 2, T], F32, "scuT", ph)
            cvT = m.sb([128, 2, T], F32, "scvT", ph)
            tmpg = m.sb([128, T], F32, "stmpg", ph)
            n = 0
            for nm, dst in (("cu", cuT), ("cv", cvT)):
                for ch in range(2):
                    u = UIDX["%s%d" % (nm, ch)]
                    pk = P[n % 2]
                    n += 1
                    proj_unit(l, u, T, pk)
                    m.act(dst[:, ch, :], pk[:, 0:T], AF.Identity, bias=LP.BIN[:, u:u + 1])
                    gelu_tanh("pool", dst[:, ch, :], dst[:, ch, :], tmpg[:, :])
            sqv = m.sb([128, 2, T], BF16, "ssqv", ph)
            m.act(sqv[:], cvT[:], AF.Square)
            for ch in range(2):
                m.mm(P[2][:, 0:T], onesB[:], sqv[:, ch, :], start=(ch == 0), stop=(ch == 1))
            rstd = m.sb([128, T], F32, "srstdv", ph)
            m.act(rstd[:], P[2][:, 0:T], AF.Sqrt, bias=epsT[:], scale=1.0 / 256)
            m.recip(rstd[:], rstd[:])
            for ch in range(2):
                m.stt(cvT[:, ch, :], cvT[:, ch, :], LP.cng[:, ch:ch + 1], rstd[:], ALU.mult, ALU.mult)
            VPA = m.sb([128, 2, 128], BF16, "sVPA", ph)
            VPB = m.sb([128, 2, 128], BF16, "sVPB", ph)
            m.memset("pool", VPA[:], 0.0)
            m.memset("pool", VPB[:], 0.0)
            tmpc = m.sb([128, 8], F32, "stmpc", ph)
            cvtok = m.sb([8, 256], F32, "cvtok", ph)
            for s in range(NSS):
                cs = slice(8 * s, 8 * s + 8)
                for ch in range(2):
                    m.tr(P[3][0:8, ch * 128:(ch + 1) * 128], cvT[:, ch, cs], identF[:])
                m.copy("act", VPA[0:8, :, 0:64], P[3][0:8, 0:256].rearrange("p (c x) -> p c x", c=2)[:, :, 0:64])
                m.copy("act", VPB[0:8, :, 64:128], P[3][0:8, 0:256].rearrange("p (c x) -> p c x", c=2)[:, :, 64:128])
                m.copy("dve", cvtok[:, :], P[3][0:8, 0:256])
                m.dma("sp", cv_s[l, 8 * s:8 * s + 8, :], cvtok[:, :])
                for ch in range(2):
                    pk = P[ch]
                    m.mm(pk[:, 0:8], VPA[0:8, ch, :], LP.WsT[0:8, 2 * ch, 0:8], start=True, stop=False)
                    m.mm(pk[:, 0:8], VPB[0:8, ch, :], LP.WsT[0:8, 2 * ch + 1, 0:8], start=False, stop=True)
                    m.tt("dve", tmpc[:, :], pk[:, 0:8], LP.BS[:, ch, 0:8], ALU.add)
                    m.tt("dve", OT[:, 6 + ch, cs], tmpc[:, :], cuT[:, ch, cs], ALU.mult)
            m.barrier()
        for oc in range(8):
            w = wload(w_out_s[l, oc])
            pk = P[oc % 2]
            for c in range(8):
                m.mm(pk[:, 0:T], w[:, c, :], OT[:, c, 0:T], start=(c == 0), stop=(c == 7))
            for (c0, n_, r) in segs:
                m.stt(xT[:, oc, c0:c0 + n_], pk[:, c0:c0 + n_], MODT[:, l, 16 + oc, r:r + 1], xT[:, oc, c0:c0 + n_], ALU.mult, ALU.add)
        norm_mod(l, A2T, 3, T, segs, hT)
        with ExitStack() as ph:
            LP.FTs = m.sb([128, 44, NSS, 2], F32, "FTs", ph)
            for s in range(NSS):
                for r_ in range(2):
                    loadT(LP.FTs[:, :, s, r_], sfconv[l, s, r_].rearrange("(c p) -> c p", p=128), 44)
            ffn(l, LP, T, segs, NSS, 8)
            for s in range(NSS):
                for r_ in range(2):
                    storeT(fconv_s[l, s, r_].rearrange("(c p) -> c p", p=128), LP.FTs[:, :, s, r_], 44)
            m.barrier()
        if l < L - 1:
            m.dma("sp", xscr_s, xT[:, :, 0:T])
        else:
            final_out(T, y_s, 0)
    chk('adaln')
    Ksel = [m.sb([65, SEQ], BF16, "Ksel%d" % g) for g in range(2)]
    Vsel = m.sb([128, NT, 2, 66], BF16, "Vsel")
    Kwin = [m.sb([65, 8 * 128], BF16, "Kwin%d" % g) for g in range(2)]
    Vwin = m.sb([128, 8, 2, 66], BF16, "Vwin")
    NBC = SEQ // 32
    NCH = (NBC + 127) // 128
    Kcmp = [m.sb([64, NBC], BF16, "Kcmp%d" % g) for g in range(2)]
    Vcmp_acc = m.sb([128, NCH, 2, 64], F32, "Vcmp_acc")
    Vcmp_bf = m.sb([128, NCH, 2, 64], BF16, "Vcmp_bf")
    for g in range(2):
        m.memset("pool", Ksel[g][:], 1.0)
        m.memset("pool", Kwin[g][:], 1.0)
    m.memset("pool", Vsel[:], 1.0)
    m.memset("pool", Vwin[:], 1.0)
    xT = m.sb([128, 8, 512], F32, "xT")
    hT = m.sb([128, 8, 512], BF16, "hT")
    OT = m.sb([128, 8, 512], BF16, "OT")
    wbuf = [m.sb([128, 8, 128], BF16, "wbuf%d" % i) for i in range(4)]
    wctr = [0]

    def wload(src, M=128):
        w = wbuf[wctr[0] % 4]
        wctr[0] += 1
        m.dma("sp", w[:, :, 0:M], src)
        return w

    def norm_mod(l, AT, kind_b, T, segs, outT):
        with ExitStack() as ph:
            sq = m.sb([128, 8, 512], BF16, "sq", ph)
            m.act(sq[:, :, 0:T], xT[:, :, 0:T], AF.Square)
            for c in range(8):
                m.mm(P[2][:, 0:T], onesB[:], sq[:, c, 0:T], start=(c == 0), stop=(c == 7))
            rstd = m.sb([128, 512], F32, "rstd", ph)
            m.act(rstd[:, 0:T], P[2][:, 0:T], AF.Sqrt, bias=epsT[:], scale=1.0 / D)
            m.recip(rstd[:, 0:T], rstd[:, 0:T])
            tmp = [m.sb([128, 512], F32, "nm_tmp%d" % i, ph) for i in range(2)]
            for c in range(8):
                tm = tmp[c % 2]
                m.tt("dve", tm[:, 0:T], xT[:, c, 0:T], rstd[:, 0:T], ALU.mult)
                for (c0, n, r) in segs:
                    if kind_b is None:
                        m.act(outT[:, c, c0:c0 + n], tm[:, c0:c0 + n], AF.Identity, bias=zero8[:, 0:1], scale=AT[:, c:c + 1])
                    else:
                        m.act(outT[:, c, c0:c0 + n], tm[:, c0:c0 + n], AF.Identity,
                              bias=MODT[:, l, kind_b * 8 + c, r:r + 1], scale=AT[:, l, c, r:r + 1])
            m.barrier()

    for l in range(L):
        lst = ExitStack()
        BIN = m.sb([128, NU], F32, "BIN", lst)
        with ExitStack() as ph:
            bstg = m.sb([NU, 128], F32, "bstg", ph)
            m.memset("dve", bstg[:], 0.0)
            for u, (nm, c0, M) in enumerate(UNITS):
                m.dma("sp", bstg[u:u + 1, 0:M], b_in[l, c0:c0 + M].rearrange("(o p) -> o p", o=1))
            m.tr(P[0][:, 0:NU], bstg[:, :], identF[0:NU, 0:NU])
            m.copy("dve", BIN[:, :], P[0][:, 0:NU])
            m.barrier()
        Atab = m.sb([128, 256], F32, "Atab", lst)
        for r4 in range(4):
            m.dma("sp", Atab[r4 * 32:(r4 + 1) * 32, :].rearrange("p (s x) -> p s x", s=2),
                  cmp_a[l].rearrange("s l g d -> l s (g d)"))
        CW = m.sb([128, 2, 64], BF16, "CW", lst)
        m.dma("pool", CW[:], cmp_w[l].rearrange("s g d e -> (g d) s e"))
        mcw = m.sb([128, 4, 2], F32, "mcw", lst)
        mcb = m.sb([128, 2], F32, "mcb", lst)
        loadT(mcw[:, :, :].rearrange("p j c -> p (j c)"), m_conv_w[l].rearrange("j (c p) -> (j c) p", p=128), 8)
        loadT(mcb[:], m_conv_b[l].rearrange("(c p) -> c p", p=128), 2)
        WQ = m.sb([128, 2, 64], BF16, "WQ", lst)
        WK = m.sb([128, 2, 64], BF16, "WK", lst)
        m.dma("pool", WQ[:], m_wq[l].rearrange("(hp hh) d e -> (hh d) hp e", hh=2))
        m.dma("pool", WK[:], m_wk[l].rearrange("(hp hh) d e -> (hh d) hp e", hh=2))
        mng = m.sb([128, 256], F32, "mng", lst)
        m.dma("sp", mng[:], m_norm_g[l].partition_broadcast(128))
        cng = m.sb([128, 2], F32, "cng", lst)
        loadT(cng[:], c_norm_g[l].rearrange("(c p) -> c p", p=128), 2)
        WsT = m.sb([128, 4, 128], BF16, "WsT", lst)
        with ExitStack() as ph:
            wsl = m.sb([128, 4, 128], F32, "wsl", ph)
            m.dma("sp", wsl[:], c_ws[l].rearrange("g t s -> t g s"))
            for g in range(4):
                m.tr(P[g % 2][:, 0:128], wsl[:, g, :], identF[:])
                m.tt("dve", WsT[:, g, :], P[g % 2][:, 0:128], tri01[:], ALU.mult)
            m.barrier()
        BS = m.sb([128, 2, 128], F32, "BS", lst)
        for g in range(4):
            m.dma("sp", BS[(g % 2) * 64:(g % 2) * 64 + 64, g // 2, :], c_bs[l, g].partition_broadcast(64))
        fcw = m.sb([128, 3, 44], F32, "fcw", lst)
        fcb = m.sb([128, 44], F32, "fcb", lst)
        for j in range(3):
            loadT(fcw[:, j, :], f_conv_w[l, j].rearrange("(c p) -> c p", p=128), 44)
        loadT(fcb[:], f_conv_b[l].rearrange("(c p) -> c p", p=128), 44)
        for t_ in (BIN, Atab, CW, mcw, mcb, WQ, WK, mng, cng, WsT, BS, fcw, fcb):
            t_.ro = True
        m.memset("dve", Vcmp_acc[:], 0.0)
        m.memset("pool", Vcmp_bf[:], 0.0)
        Caug = m.sb([64, 4, 65], F32, "Caug", lst)
        Caug_bf = m.sb([64, 4, 66], BF16, "Caug_bf", lst)
        m.memset("dve", Caug[:], 0.0)
        m.memset("pool", Caug_bf[:], 0.0)
        mstate = m.sb([4, 1], F32, "mstate", lst)
        m.memset("dve", mstate[:], 0.0)
        MQ = m.sb([128, 2, 515], F32, "MQ", lst)
        m.memset("dve", MQ[:, :, 0:3], 0.0)
        FT = m.sb([128, 44, 2], F32, "FT", lst)
        m.memset("dve", FT[:], 0.0)
        m.barrier()

        chk('layerparams')
        LP = NS()
        for nm_ in ("BIN", "Atab", "CW", "mcw", "mcb", "WQ", "WK", "mng", "cng", "WsT", "BS", "fcw", "fcb", "MQ", "FT"):
            setattr(LP, nm_, locals()[nm_])
        S = NS()
        S.Caug, S.Caug_bf, S.mstate = Caug, Caug_bf, mstate
        if cfg.do_prompt:
            for gi in range(NG):
                prompt_group(l, gi, LP)
                prompt_group2(l, gi, LP, S)
            for h in range(4):
                storeT(mC_p[l, h], Caug[:, h, 0:64], 64, 64)
                storeT(mn_p[l, h:h + 1, :], Caug[:, h, 64:65], 1, 64)
            m.dma("sp", mm_p[l].rearrange("(h o) -> h o", o=1), mstate[:, 0:1], allow_slow_non_contiguous=True)
        if cfg.do_sample:
            m.barrier()
            sample_layer(l, LP)
        lst.close()
        m.barrier()

    m.finish()
    st.close()
    return B, m


import numpy as np
from concourse.bass_utils import run_bass_kernel_spmd

WNAMES = ["ada_w", "ada_b", "norm1_g", "norm2_g", "w_in", "b_in", "cmp_a", "cmp_w", "m_conv_w", "m_conv_b", "m_wq", "m_wk",
          "c_norm_g", "c_ws", "c_bs", "w_out", "w_up", "f_conv_w", "f_conv_b", "w_down", "final_norm_g"]


def run_kernel(cfg, inp, ncores=8, trace=False):
    f = lambda a: np.ascontiguousarray(np.asarray(a, dtype=np.float32))
    L, NSS = cfg.DEPTH, cfg.NSS
    B, m = build_program(cfg)
    hc = host_consts(cfg)
    BATCH = inp["x_prompt"].shape[0]
    shared = {k: f(inp[k]) for k in WNAMES}
    shared["m_norm_g"] = f(inp["m_norm_g"]).reshape(L, 256)
    shared["cache"] = f(inp["cache_nsa_kv"]).reshape(L, -1, 512)
    for k, v in hc.items():
        shared["c_" + k] = v
    in_maps = []
    for c in range(ncores):
        b = c % BATCH
        ss = slice(c * NSS, (c + 1) * NSS)
        d = dict(shared)
        d["xp"] = f(inp["x_prompt"][b])
        d["xs"] = f(inp["x_sample"][ss]).reshape(NSS * 8, D)
        d["ptab"] = np.ascontiguousarray(np.asarray(inp["page_table"][ss], dtype=np.int32))
        d["wkv"] = f(inp["cache_win_kv"][:, ss]).reshape(L, NSS, 512, 256)
        d["sC"] = f(inp["state_mlstm_C"][:, ss])
        d["sn"] = f(inp["state_mlstm_n"][:, ss])
        d["sm"] = f(inp["state_mlstm_m"][:, ss])
        d["sconv"] = f(inp["state_mlstm_conv"][:, ss])
        d["sfconv"] = f(inp["state_ffn_conv"][:, ss])
        d["cc"] = np.concatenate([f(inp["c_prompt"][b:b + 1]), f(inp["c_sample"][ss])], 0)
        in_maps.append(d)
    res = run_bass_kernel_spmd(B.nc, in_maps, core_ids=list(range(ncores)), trace=trace)
    R = res.results
    SEQ = cfg.SEQ
    nb = min(BATCH, ncores)
    cat = lambda name, ax, n=ncores: np.concatenate([R[c][name] for c in range(n)], axis=ax)
    stk = lambda name, ax: np.stack([R[c][name] for c in range(nb)], axis=ax)
    out = (
        stk("y_p", 0),
        cat("y_s", 0).reshape(-1, 8, D),
        stk("kvr_p", 1).reshape(L, nb, SEQ, 4, 2, 64),
        cat("kvr_s", 1).reshape(L, -1, 8, 4, 2, 64),
        stk("win_p", 1).reshape(L, nb, 512, 2, 2, 64),
        cat("win_s", 1).reshape(L, -1, 512, 2, 2, 64),
        stk("mC_p", 1), stk("mn_p", 1), stk("mm_p", 1), stk("mconv_p", 1),
        cat("mC_s", 1), cat("mn_s", 1), cat("mm_s", 1), cat("mconv_s", 1),
        cat("cv_s", 1).reshape(L, -1, 8, 256),
        stk("fconv_p", 1), cat("fconv_s", 1),
    )
    return out, res


def kernel(**inputs):
    cfg = Cfg(SEQ=8192, PAST=16384, NSS=4, NPOOL=int(np.asarray(inputs["cache_nsa_kv"]).shape[1]), DEPTH=2)
    out, _ = run_kernel(cfg, inputs, ncores=8)
    return tuple(np.ascontiguousarray(o, dtype=np.float32) for o in out)
```
